# Optimizing a Trainium2 kernel written in Bass

```python
import jax, jax.numpy as jnp
from jax import lax
import numpy as np

D_MODEL = 4096
BATCH = 16
SEQ = 256
DEPTH = 4
DEC_BATCH = 8
DEC_SEQ = 4096
PAST_LEN = 256

GRID_W = 64
HEAD_DIM = 128
EPS = 1e-6
A_GROUPS = 4
A_CH = 128
A_WIDTH = A_GROUPS * A_CH
A_CHUNK = 128
B_HEADS = 4
B_DK = 128
B_DV = 128
B_KW = B_HEADS * B_DK
B_VW = B_HEADS * B_DV
B_CONV = 5
B_CHUNK = 64
C_HEADS = 8
C_KV = 2
C_QW = C_HEADS * HEAD_DIM
C_KVW = C_KV * HEAD_DIM
Q_BLOCK = 128
ROPE_THETA = 10000.0
N_BRANCH = 3
GATE_RANK = 256
FFN_BASE = D_MODEL // 2
FFN = -(-8 * FFN_BASE // (3 * 256)) * 256
IN_WIDTHS = (A_WIDTH, A_WIDTH, 2 * B_KW + B_VW, B_VW, 4 * B_HEADS, C_QW, 2 * C_KVW, GATE_RANK)
N_IN = sum(IN_WIDTHS)
IN_SPLITS = tuple(int(s) for s in np.cumsum(IN_WIDTHS)[:-1])

kernel_name = "hybrid_gmlp_deltanet_gqa_diffusion_step"


def rms_norm(x):
    xf = x.astype(jnp.float32)
    return (xf * lax.rsqrt(jnp.mean(xf * xf, axis=-1, keepdims=True) + EPS)).astype(x.dtype)


def l2_norm(x):
    xf = x.astype(jnp.float32)
    return (xf * lax.rsqrt(jnp.sum(xf * xf, axis=-1, keepdims=True) + EPS)).astype(x.dtype)


def modulate(x, shift, scale):
    return rms_norm(x) * (1.0 + scale) + shift


def ada_params(cond, w_ada, b_ada):
    m = jax.nn.silu(cond) @ w_ada + b_ada
    return m.reshape(cond.shape[0], 6, 1, D_MODEL)


def axial_rope_tables(n_tokens):
    rows = n_tokens // GRID_W
    row = jnp.repeat(jnp.arange(rows, dtype=jnp.float32), GRID_W)
    col = jnp.tile(jnp.arange(GRID_W, dtype=jnp.float32), rows)
    axis_dim = HEAD_DIM // 2
    inv = 1.0 / (ROPE_THETA ** (jnp.arange(0, axis_dim, 2, dtype=jnp.float32) / axis_dim))
    ang = jnp.stack([row[:, None] * inv, col[:, None] * inv], axis=1)
    return jnp.cos(ang), jnp.sin(ang)


def apply_axial_rope(x, cos, sin):
    B, L, H, D = x.shape
    xr = x.astype(jnp.float32).reshape(B, L, H, 2, 2, D // 4)
    x1, x2 = xr[..., 0, :], xr[..., 1, :]
    c = cos[None, :, None]
    s = sin[None, :, None]
    out = jnp.stack([x1 * c - x2 * s, x2 * c + x1 * s], axis=-2)
    return out.reshape(B, L, H, D).astype(x.dtype)


def gqa_attend(q, k, v):
    B, L, H, D = q.shape
    G = H // C_KV
    nb = L // Q_BLOCK
    qb = q.reshape(B, nb, Q_BLOCK, C_KV, G, D).transpose(1, 0, 2, 3, 4, 5)

    def one_block(q_blk):
        s = jnp.einsum('bqkgd,bskd->bkgqs', q_blk, k, preferred_element_type=jnp.float32)
        p = jax.nn.softmax(s, axis=-1).astype(v.dtype)
        return jnp.einsum('bkgqs,bskd->bqkgd', p, v)

    o = lax.map(one_block, qb)
    return o.transpose(1, 0, 2, 3, 4, 5).reshape(B, L, H * D)


def chunk_mlp_branch(u, v, v_gain, w_s, b_s):
    u = jax.nn.gelu(u)
    v = rms_norm(jax.nn.gelu(v)) * v_gain
    B, L, _ = v.shape
    vc = v.reshape(B, L // A_CHUNK, A_CHUNK, A_GROUPS, A_CH)
    mixed = jnp.einsum('gpq,bnqgc->bnpgc', w_s, vc) + b_s.T[None, None, :, :, None]
    return u * mixed.reshape(B, L, A_WIDTH)


def centred_depthwise_conv(x, w):
    K = w.shape[0]
    return lax.conv_general_dilated(x, w[:, None, :].astype(x.dtype), window_strides=(1,),
                                    padding=((K // 2, K // 2),), dimension_numbers=('NWC', 'WIO', 'NWC'),
                                    feature_group_count=x.shape[-1])


def gated_delta_chunked(q, k, v, g, beta, s0):
    B, L, H, DK = q.shape
    DV = v.shape[-1]
    C = B_CHUNK
    N = L // C

    def chunks(t):
        t = t.astype(jnp.float32).reshape((B, N, C, H) + t.shape[3:])
        return jnp.moveaxis(t, 3, 1)

    qc, kc, vc, gc, bc = chunks(q), chunks(k), chunks(v), chunks(g), chunks(beta)
    Gc = jnp.cumsum(gc, axis=-1)
    diff = Gc[..., :, None] - Gc[..., None, :]
    tri_incl = jnp.tril(jnp.ones((C, C), bool))
    tri_strict = jnp.tril(jnp.ones((C, C), bool), -1)
    decay = jnp.where(tri_incl, jnp.exp(jnp.where(tri_incl, diff, 0.0)), 0.0)
    kk = jnp.einsum('bhntd,bhnjd->bhntj', kc, kc)
    Lmat = jnp.where(tri_strict, bc[..., :, None] * kk * decay, 0.0)
    gam = jnp.exp(Gc)
    rhs = jnp.concatenate([(bc * gam)[..., None] * kc, bc[..., None] * vc], axis=-1)
    sol = lax.linalg.triangular_solve(jnp.eye(C, dtype=jnp.float32) + Lmat, rhs,
                                      left_side=True, lower=True, unit_diagonal=True)
    Wm, Uv = sol[..., :DK], sol[..., DK:]
    qk = jnp.einsum('bhntd,bhnjd->bhntj', qc, kc) * decay
    q_dec = gam[..., None] * qc
    k_dec = jnp.exp(Gc[..., -1:] - Gc)[..., None] * kc
    gam_end = gam[..., -1]
    xs = tuple(jnp.moveaxis(t, 2, 0) for t in (Wm, Uv, qk, q_dec, k_dec, gam_end))

    def step(S, inp):
        w_n, uv_n, qk_n, qd_n, kd_n, ge_n = inp
        U = uv_n - jnp.einsum('bhcd,bhde->bhce', w_n, S)
        O = jnp.einsum('bhcd,bhde->bhce', qd_n, S) + jnp.einsum('bhtj,bhje->bhte', qk_n, U)
        S = ge_n[..., None, None] * S + jnp.einsum('bhcd,bhce->bhde', kd_n, U)
        return S, O

    S_fin, O = lax.scan(step, s0.astype(jnp.float32), xs)
    O = O.transpose(1, 0, 3, 2, 4).reshape(B, L, H, DV)
    return O.astype(v.dtype), S_fin


def deltanet_branch(qkv, gate, ab, conv_w, a_log, dt_bias, o_gain, s0f, s0b):
    B, L, _ = qkv.shape
    qkv = jax.nn.silu(centred_depthwise_conv(qkv, conv_w))
    q, k, v = jnp.split(qkv, [B_KW, 2 * B_KW], axis=-1)
    q = l2_norm(q.reshape(B, L, B_HEADS, B_DK)) * (B_DK ** -0.5)
    k = l2_norm(k.reshape(B, L, B_HEADS, B_DK))
    v = v.reshape(B, L, B_HEADS, B_DV)
    a_f, a_b, b_f, b_b = jnp.split(ab.astype(jnp.float32), 4, axis=-1)
    a_log = a_log.astype(jnp.float32)
    dt_bias = dt_bias.astype(jnp.float32)
    g_f = -jnp.exp(a_log[0]) * jax.nn.softplus(a_f + dt_bias[0])
    g_b = -jnp.exp(a_log[1]) * jax.nn.softplus(a_b + dt_bias[1])
    o_f, s_f = gated_delta_chunked(q, k, v, g_f, jax.nn.sigmoid(b_f), s0f)
    rev = lambda t: jnp.flip(t, axis=1)
    o_b, s_b = gated_delta_chunked(rev(q), rev(k), rev(v), rev(g_b), rev(jax.nn.sigmoid(b_b)), s0b)
    o = o_f + rev(o_b)
    o = rms_norm(o) * o_gain * jax.nn.silu(gate.reshape(B, L, B_HEADS, B_DV))
    return o.reshape(B, L, B_VW), s_f, s_b


def attention_branch(q, k, v, q_gain, k_gain, ctx_kv):
    B, L, _ = q.shape
    q = rms_norm(q.reshape(B, L, C_HEADS, HEAD_DIM)) * q_gain
    k = rms_norm(k.reshape(B, L, C_KV, HEAD_DIM)) * k_gain
    v = v.reshape(B, L, C_KV, HEAD_DIM)
    if ctx_kv is None:
        k_all, v_all = k, v
        new_kv = (k, v)
    else:
        cos, sin = axial_rope_tables(L)
        q = apply_axial_rope(q, cos, sin)
        k = apply_axial_rope(k, cos, sin)
        k_all = jnp.concatenate([ctx_kv[0].astype(k.dtype), k], axis=1)
        v_all = jnp.concatenate([ctx_kv[1].astype(v.dtype), v], axis=1)
        new_kv = None
    o = gqa_attend(q * (HEAD_DIM ** -0.5), k_all, v_all)
    return o, new_kv


def trunk_layer(x, mod, w_in, a_v_gain, a_w_s, a_b_s, b_conv, b_a_log, b_dt_bias, b_o_gain,
                c_q_gain, c_k_gain, w_mg, w_br_a, w_br_b, w_br_c, w_o, w_gate, w_up, w_down, ctx):
    B, L, _ = x.shape
    h = modulate(x, mod[:, 0], mod[:, 1])
    proj = h @ w_in
    u_a, v_a, qkv_b, gate_b, ab_b, q_c, kv_c, gate_lr = jnp.split(proj, IN_SPLITS, axis=-1)
    k_c, v_c = jnp.split(kv_c, 2, axis=-1)
    if ctx is None:
        s0f = jnp.zeros((B, B_HEADS, B_DK, B_DV), jnp.float32)
        s0b = s0f
        ctx_kv = None
    else:
        s0f, s0b = ctx[2], ctx[3]
        ctx_kv = (ctx[0], ctx[1])
    y_a = chunk_mlp_branch(u_a, v_a, a_v_gain, a_w_s, a_b_s)
    y_b, s_f, s_b = deltanet_branch(qkv_b, gate_b, ab_b, b_conv, b_a_log, b_dt_bias, b_o_gain, s0f, s0b)
    y_c, new_kv = attention_branch(q_c, k_c, v_c, c_q_gain, c_k_gain, ctx_kv)
    g_logit = (gate_lr @ w_mg).astype(jnp.float32)
    g = jax.nn.sigmoid(g_logit).astype(x.dtype).reshape(B, L, N_BRANCH, D_MODEL)
    merged = g[:, :, 0] * (y_a @ w_br_a) + g[:, :, 1] * (y_b @ w_br_b) + g[:, :, 2] * (y_c @ w_br_c)
    x = x + mod[:, 2] * (merged @ w_o)
    h = modulate(x, mod[:, 3], mod[:, 4])
    x = x + mod[:, 5] * ((jax.nn.silu(h @ w_gate) * (h @ w_up)) @ w_down)
    if ctx is None:
        return x, (new_kv[0], new_kv[1], s_f, s_b)
    return x, None


def setup_inputs(seed: int = 0) -> dict:
    key = jax.random.key(seed)
    ks = jax.random.split(key, 32)
    D = D_MODEL

    def nrm(k, shape, s):
        return jax.random.normal(k, shape, jnp.float32) * s

    return dict(
        x_prompt=nrm(ks[0], (BATCH, SEQ, D), 1.0),
        x_sample=nrm(ks[1], (DEC_BATCH, DEC_SEQ, D), 1.0),
        cache_k=nrm(ks[2], (DEC_BATCH, DEPTH, PAST_LEN, C_KV, HEAD_DIM), 1.0),
        cache_v=nrm(ks[3], (DEC_BATCH, DEPTH, PAST_LEN, C_KV, HEAD_DIM), 1.0),
        state_fwd=nrm(ks[4], (DEC_BATCH, DEPTH, B_HEADS, B_DK, B_DV), B_DK ** -0.5),
        state_bwd=nrm(ks[5], (DEC_BATCH, DEPTH, B_HEADS, B_DK, B_DV), B_DK ** -0.5),
        c=nrm(ks[6], (DEC_BATCH, D), 1.0),
        c_ctx=nrm(ks[7], (D,), 1.0),
        w_ada=nrm(ks[8], (DEPTH, D, 6 * D), 0.5 * D ** -0.5),
        b_ada=nrm(ks[9], (DEPTH, 6 * D), 0.01),
        w_in=nrm(ks[10], (DEPTH, D, N_IN), D ** -0.5),
        a_v_gain=1.0 + nrm(ks[11], (DEPTH, A_WIDTH), 0.01),
        a_w_s=nrm(ks[12], (DEPTH, A_GROUPS, A_CHUNK, A_CHUNK), A_CHUNK ** -0.5),
        a_b_s=1.0 + nrm(ks[13], (DEPTH, A_GROUPS, A_CHUNK), 0.01),
        b_conv=nrm(ks[14], (DEPTH, B_CONV, 2 * B_KW + B_VW), B_CONV ** -0.5),
        b_a_log=jnp.log(jax.random.uniform(ks[15], (DEPTH, 2, B_HEADS), jnp.float32, 1.0, 16.0)),
        b_dt_bias=nrm(ks[16], (DEPTH, 2, B_HEADS), 0.1),
        b_o_gain=1.0 + nrm(ks[17], (DEPTH, B_DV), 0.01),
        c_q_gain=1.0 + nrm(ks[18], (DEPTH, HEAD_DIM), 0.01),
        c_k_gain=1.0 + nrm(ks[19], (DEPTH, HEAD_DIM), 0.01),
        w_mg=nrm(ks[20], (DEPTH, GATE_RANK, N_BRANCH * D), GATE_RANK ** -0.5),
        w_br_a=nrm(ks[21], (DEPTH, A_WIDTH, D), A_WIDTH ** -0.5),
        w_br_b=nrm(ks[22], (DEPTH, B_VW, D), B_VW ** -0.5),
        w_br_c=nrm(ks[23], (DEPTH, C_QW, D), C_QW ** -0.5),
        w_o=nrm(ks[24], (DEPTH, D, D), D ** -0.5),
        w_gate=nrm(ks[25], (DEPTH, D, FFN), D ** -0.5),
        w_up=nrm(ks[26], (DEPTH, D, FFN), D ** -0.5),
        w_down=nrm(ks[27], (DEPTH, FFN, D), FFN ** -0.5),
    )


def reference(x_prompt, x_sample, cache_k, cache_v, state_fwd, state_bwd, c, c_ctx,
              w_ada, b_ada, w_in, a_v_gain, a_w_s, a_b_s, b_conv, b_a_log, b_dt_bias, b_o_gain,
              c_q_gain, c_k_gain, w_mg, w_br_a, w_br_b, w_br_c, w_o, w_gate, w_up, w_down):
    yp = x_prompt
    ys = x_sample
    ks_, vs_, sfs_, sbs_ = [], [], [], []
    for l in range(DEPTH):
        lw = (w_in[l], a_v_gain[l], a_w_s[l], a_b_s[l], b_conv[l], b_a_log[l], b_dt_bias[l], b_o_gain[l],
              c_q_gain[l], c_k_gain[l], w_mg[l], w_br_a[l], w_br_b[l], w_br_c[l], w_o[l],
              w_gate[l], w_up[l], w_down[l])
        mod_p = ada_params(c_ctx[None, :], w_ada[l], b_ada[l])
        yp, (k_l, v_l, sf_l, sb_l) = trunk_layer(yp, mod_p, *lw, None)
        ks_.append(k_l)
        vs_.append(v_l)
        sfs_.append(sf_l)
        sbs_.append(sb_l)
        mod_s = ada_params(c, w_ada[l], b_ada[l])
        ys, _ = trunk_layer(ys, mod_s, *lw, (cache_k[:, l], cache_v[:, l], state_fwd[:, l], state_bwd[:, l]))
    new_cache_k = jnp.stack(ks_, axis=1)
    new_cache_v = jnp.stack(vs_, axis=1)
    new_state_fwd = jnp.stack(sfs_, axis=1)
    new_state_bwd = jnp.stack(sbs_, axis=1)
    return (yp, ys, new_cache_k, new_cache_v, new_state_fwd, new_state_bwd)
```

```python
import numpy as np
from contextlib import ExitStack
import concourse.bass as bass
import concourse.mybir as mybir
from concourse.bass_utils import run_bass_kernel_spmd

F32 = mybir.dt.float32
BF16 = mybir.dt.bfloat16
AF = mybir.ActivationFunctionType
ALU = mybir.AluOpType

D = 4096
KC = 32
N_IN = 4880
FFN = 5632
FC = 44
LP = 256
EPS = 1e-6
DM = 8


class Res:
    __slots__ = ("w", "r")

    def __init__(self):
        self.w = {}
        self.r = {}


class KB:
    def __init__(self, nc, es):
        self.nc = nc
        self.E = {"pe": nc.tensor, "act": nc.scalar, "dve": nc.vector, "pool": nc.gpsimd, "sp": nc.sync}
        self.csem = {e: es.enter_context(nc.semaphore("c_" + e)) for e in ("pe", "act", "dve", "pool")}
        self.ccnt = {e: 0 for e in self.csem}
        self.dsem = {q: [es.enter_context(nc.semaphore("d_%s%d" % (q, i))) for i in range(DM)] for q in ("sp", "pool")}
        self.dcnt = {q: 0 for q in self.dsem}
        self.waited = {e: {} for e in self.E}
        self.nbank = 0

    def _wait(self, e, sem, val):
        k = id(sem)
        if self.waited[e].get(k, 0) >= val:
            return
        self.waited[e][k] = val
        self.E[e].wait_ge(sem, val)

    def _deps(self, e, reads, writes, own_sem, disjoint=False):
        evs = {}

        def add(d):
            for k, (sem, val) in d.items():
                if e == "pe" and sem is own_sem:
                    continue
                if k not in evs or evs[k][1] < val:
                    evs[k] = (sem, val)

        for r in reads:
            add(r.w)
        for w in writes:
            add(w.r)
            if not disjoint:
                add(w.w)
        for sem, val in evs.values():
            self._wait(e, sem, val)

    @staticmethod
    def _rec(reads, writes, sem, val):
        k = id(sem)
        for r in reads:
            r.r[k] = (sem, val)
        for w in writes:
            w.w[k] = (sem, val)

    def op(self, e, fn, reads=(), writes=()):
        sem = self.csem[e]
        self._deps(e, reads, writes, sem)
        ins = fn(self.E[e])
        self.ccnt[e] += 1
        ins.then_inc(sem, 1)
        self._rec(reads, writes, sem, self.ccnt[e])

    def dma(self, q, out, in_, reads=(), writes=(), disjoint=False):
        i = self.dcnt[q]
        slot, gen = i % DM, i // DM
        sem = self.dsem[q][slot]
        self._deps(q, reads, writes, None, disjoint)
        if gen > 0:
            self._wait(q, sem, 16 * gen)
        self.E[q].dma_start(out=out, in_=in_).then_inc(sem, 16)
        self.dcnt[q] += 1
        self._rec(reads, writes, sem, 16 * (gen + 1))

    def barrier(self):
        evs = []
        for e, sem in self.csem.items():
            if self.ccnt[e]:
                evs.append((sem, self.ccnt[e]))
        for q, sems in self.dsem.items():
            n = self.dcnt[q]
            for s in range(DM):
                cnt = (n - s + DM - 1) // DM if n > s else 0
                if cnt:
                    evs.append((sems[s], 16 * cnt))
        for e in self.E:
            for sem, val in evs:
                self._wait(e, sem, val)


class Cfg:
    def __init__(self, depth=4, LS=4096, NP=2, debug=False, nphase=99):
        self.depth, self.LS, self.NP, self.debug, self.nphase = depth, LS, NP, debug, nphase
        self.LPT = NP * LP
        self.LT = LS + self.LPT
        self.tiles = []
        c = 0
        while c < LS:
            n = min(1024, LS - c)
            self.tiles.append((c, n, True))
            c += n
        if self.LPT:
            self.tiles.append((LS, self.LPT, False))
        self.sub = []
        for (c0, n, s) in self.tiles:
            for j in range(n // 512):
                self.sub.append((c0 + j * 512, 512, s))


def build(cfg):
    nc = bass.Bass("TRN2", target_bir_lowering=False)
    es = ExitStack()
    L, LS, LT, NPR = cfg.depth, cfg.LS, cfg.LT, cfg.NP

    def din(name, shape, dt=F32):
        return nc.dram_tensor(name, list(shape), dt, kind="ExternalInput").ap()

    def dout(name, shape, dt=F32):
        return nc.dram_tensor(name, list(shape), dt, kind="ExternalOutput").ap()

    def dscr(name, shape, dt=F32):
        kind = "ExternalOutput" if cfg.debug else "Internal"
        return nc.dram_tensor(name, list(shape), dt, kind=kind).ap()

    xs = din("xs", [LS, D])
    xp = din("xp", [cfg.LPT, D])
    condT = din("condT", [128, KC, 2])
    w_ada = din("w_ada", [L, D, 6 * D])
    b_adaT = din("b_adaT", [L, 128, 192])
    w_in = din("w_in", [L, D, N_IN])
    w_mg = din("w_mg", [L, 256, 3 * D])
    w_br_a = din("w_br_a", [L, 512, D])
    w_br_b = din("w_br_b", [L, 512, D])
    w_br_c = din("w_br_c", [L, 1024, D])
    w_o = din("w_o", [L, D, D])
    w_gate = din("w_gate", [L, D, FFN])
    w_up = din("w_up", [L, D, FFN])
    w_down = din("w_down", [L, FFN, D])
    ident_d = din("ident", [128, 128])
    a_wsT = din("a_wsT", [L, 128, 4, 128])
    bs_row = din("bs_row", [L, 1, 512])
    pv = din("pv", [L, 128, 8])
    ropeC = din("ropeC", [128, LS])
    ropeS = din("ropeS", [128, LS])
    perm_d = din("perm", [128, 128])
    ck = din("ck", [L, 256, 256])
    cv = din("cv", [L, 256, 256])
    b_convT = din("b_convT", [L, 128, 12, 5])
    ab_par = din("ab_par", [L, 16, 2])
    sf_in = din("sf_in", [L, 4, 128, 128])
    sb_in = din("sb_in", [L, 4, 128, 128])
    dmask = din("dmask", [7, 128, 512])
    new_sf = dout("new_sf", [cfg.NP, L, 4, 128, 128])
    new_sb = dout("new_sb", [cfg.NP, L, 4, 128, 128])
    new_k = dout("new_k", [cfg.NP, L, 256, 256])
    new_v = dout("new_v", [cfg.NP, L, 256, 256])

    ys = dout("ys", [LS, D])
    yp = dout("yp", [cfg.LPT, D])

    xT = dscr("xT", [D, LT])
    hT = dscr("hT", [D, LT], BF16)
    projT = dscr("projT", [N_IN, LT])
    yaT = dscr("yaT", [512, LT], BF16)
    ybT = dscr("ybT", [512, LT], BF16)
    ycT = dscr("ycT", [1024, LT], BF16)
    mrgT = dscr("mrgT", [D, LT], BF16)
    hidT = dscr("hidT", [FFN, LT], BF16)
    qnT = dscr("qnT", [1024, LT], BF16)
    NCH = LT // 128
    dn_scr = dscr("dn_scr", [NCH, 2, 5, 128, 512])
    dn_ge = dscr("dn_ge", [NCH, 2, 128, 4])
    Osc = dscr("Osc", [2, LT, 512])

    K = KB(nc, es)
    uid = [0]

    def sb(name, shape, dt=F32, st=es):
        uid[0] += 1
        return st.enter_context(nc.sbuf_tensor("%s_%d" % (name, uid[0]), list(shape), dt))

    psum = [es.enter_context(nc.psum_tensor("ps%d" % i, [128, 512], F32)) for i in range(8)]
    psr = [Res() for _ in range(8)]
    NWB = 4
    wbuf = [None] * NWB
    wres = [None] * NWB
    wcnt = [0]

    def walloc(ph):
        for i in range(NWB):
            wbuf[i] = sb("wbuf%d" % i, [128, 32 * 256], BF16, ph)
            wres[i] = Res()
    ident = sb("ident_sb", [128, 128])
    onesD = sb("onesD", [128, 128], BF16)
    modT = sb("modT", [128, 192, 2])
    scond = sb("scond", [128, KC, 2], BF16)
    cres = Res()
    modres = Res()

    def bank():
        b = K.nbank % 8
        K.nbank += 1
        return b

    def wnext():
        i = wcnt[0] % NWB
        wcnt[0] += 1
        return i

    def wview(i, nk, ncols):
        return wbuf[i][:, 0:nk * ncols].rearrange("p (k c) -> p k c", c=ncols)

    def wload(i, wmat, k0, nk, c0, ncols, slot0=0, view_cols=None):
        vc = view_cols or ncols
        src = wmat[k0 * 128:(k0 + nk) * 128, c0:c0 + ncols].rearrange("(k p) n -> p k n", p=128)
        dst = wbuf[i][:, slot0 * vc:(slot0 + nk) * vc].rearrange("p (k c) -> p k c", c=vc)[:, :, 0:ncols]
        K.dma("pool", dst, src, writes=[wres[i]], disjoint=(slot0 != 0))

    with ExitStack() as ph:
        tmp = sb("c_tmp", [128, KC, 2], F32, ph)
        tmp2 = sb("c_tmp2", [128, KC, 2], F32, ph)
        tr = Res()
        K.dma("sp", ident[:], ident_d[:, :], writes=[cres])
        K.dma("sp", tmp[:], condT[:, :, :], writes=[tr])
        K.op("pool", lambda e: e.memset(onesD[:], 1.0 / D), writes=[cres])
        K.op("act", lambda e: e.activation(out=tmp2[:], in_=tmp[:], func=AF.Sigmoid), reads=[tr], writes=[tr])
        K.op("dve", lambda e: e.tensor_tensor(out=scond[:], in0=tmp[:], in1=tmp2[:], op=ALU.mult), reads=[tr], writes=[cres])
        K.barrier()

    def phase_transpose_in():
        with ExitStack() as ph:
            xin = [sb("xin%d" % i, [128, D], F32, ph) for i in range(2)]
            xres = [Res() for _ in range(2)]
            stg = sb("xstg", [128, KC, 512], F32, ph)
            sres = [Res() for _ in range(4)]
            nblk = LT // 128
            for sidx, (c0, ncols, is_s) in enumerate(cfg.sub):
                for j in range(4):
                    blk = sidx * 4 + j
                    tok0 = c0 + j * 128
                    src = xs[tok0:tok0 + 128, :] if is_s else xp[tok0 - LS:tok0 - LS + 128, :]
                    xb = blk % 2
                    K.dma("sp", xin[xb][:], src, writes=[xres[xb]])
                    for g in range(8):
                        b = bank()
                        for i in range(4):
                            kc = g * 4 + i
                            K.op("pe", lambda e, b=b, i=i, kc=kc, xb=xb: e.transpose(
                                psum[b][:, i * 128:(i + 1) * 128], xin[xb][:, kc * 128:(kc + 1) * 128], ident[:]),
                                reads=[xres[xb], cres], writes=[psr[b]])
                        eng = "act" if g % 2 == 0 else "dve"
                        dst = stg[:, g * 4:(g + 1) * 4, j * 128:(j + 1) * 128]
                        srcp = psum[b][:, :].rearrange("p (k t) -> p k t", t=128)
                        if eng == "act":
                            K.op("act", lambda e, dst=dst, srcp=srcp: e.activation(out=dst, in_=srcp, func=AF.Copy),
                                 reads=[psr[b]], writes=[sres[j]])
                        else:
                            K.op("dve", lambda e, dst=dst, srcp=srcp: e.tensor_copy(out=dst, in_=srcp),
                                 reads=[psr[b]], writes=[sres[j]])
                K.dma("sp", xT.rearrange("(k p) t -> p k t", p=128)[:, :, c0:c0 + 512], stg[:], reads=sres)
            K.barrier()

    def phase_mod(l):
        with ExitStack() as ph:
            walloc(ph)
            bt = sb("badaT", [128, 192], F32, ph)
            br = Res()
            K.dma("sp", bt[:], b_adaT[l], writes=[br])
            b = bank()
            ps3 = psum[b][:, 0:384].rearrange("p (i r) -> p i r", r=2)
            for blk in range(96):
                wi = wnext()
                wload(wi, w_ada[l], 0, KC, blk * 256, 256)
                wv = wview(wi, KC, 256)
                for m in range(2):
                    idx = blk * 2 + m
                    for kc in range(KC):
                        K.op("pe", lambda e, wv=wv, m=m, kc=kc, idx=idx: e.matmul(
                            ps3[:, idx, :], wv[:, kc, m * 128:(m + 1) * 128], scond[:, kc, :],
                            start=(kc == 0), stop=(kc == KC - 1)),
                            reads=[wres[wi], cres], writes=[psr[b]])
            for r in range(2):
                K.op("dve", lambda e, r=r: e.tensor_tensor(out=modT[:, :, r], in0=ps3[:, :, r], in1=bt[:], op=ALU.add),
                     reads=[psr[b], br], writes=[modres])
            for j in (1, 4):
                K.op("dve", lambda e, j=j: e.tensor_scalar(
                    out=modT[:, j * 32:(j + 1) * 32, :], in0=modT[:, j * 32:(j + 1) * 32, :],
                    scalar1=1.0, scalar2=None, op0=ALU.add), reads=[modres], writes=[modres])
            K.barrier()

    def modcol(j, kc, is_s):
        r = 1 if is_s else 0
        return modT[:, j * 32 + kc, r:r + 1]

    def phase_norm(jshift, jscale):
        with ExitStack() as ph:
            xt = sb("n_x", [128, KC, 512], F32, ph)
            xr = [Res() for _ in range(4)]
            sq = [sb("n_sq%d" % i, [128, 512], BF16, ph) for i in range(3)]
            sqr = [Res() for _ in range(3)]
            lnt = sb("n_ln", [128, 512], F32, ph)
            rstd = sb("n_rstd", [128, 512], F32, ph)
            rr = Res()
            tmpb = [sb("n_tmp%d" % i, [128, 512], F32, ph) for i in range(3)]
            tmr = [Res() for _ in range(3)]
            ho = [sb("n_ho%d" % i, [128, 8, 512], BF16, ph) for i in range(2)]
            hr = [Res() for _ in range(2)]
            epsc = sb("n_eps", [128, 1], F32, ph)
            K.op("pool", lambda e: e.memset(epsc[:], EPS), writes=[rr])
            xTv = xT.rearrange("(k p) t -> p k t", p=128)
            hTv = hT.rearrange("(k p) t -> p k t", p=128)
            n = 0
            hn = 0
            for (c0, ncols, is_s) in cfg.sub:
                for g in range(4):
                    K.dma("sp", xt[:, g * 8:(g + 1) * 8, :], xTv[:, g * 8:(g + 1) * 8, c0:c0 + 512], writes=[xr[g]])
                b = bank()
                for kc in range(KC):
                    s = n % 3
                    n += 1
                    K.op("act", lambda e, s=s, kc=kc: e.activation(out=sq[s][:], in_=xt[:, kc, :], func=AF.Square),
                         reads=[xr[kc // 8]], writes=[sqr[s]])
                    K.op("pe", lambda e, s=s, kc=kc, b=b: e.matmul(psum[b][:, :], onesD[:], sq[s][:],
                                                                   start=(kc == 0), stop=(kc == KC - 1)),
                         reads=[sqr[s], cres], writes=[psr[b]])
                K.op("act", lambda e, b=b: e.activation(out=lnt[:], in_=psum[b][:, :], func=AF.Ln, bias=epsc[:]),
                     reads=[psr[b], rr], writes=[rr])
                K.op("act", lambda e: e.activation(out=rstd[:], in_=lnt[:], func=AF.Exp, scale=-0.5),
                     reads=[rr], writes=[rr])
                for g in range(4):
                    hb = hn % 2
                    hn += 1
                    for i in range(8):
                        kc = g * 8 + i
                        s = n % 3
                        n += 1
                        K.op("dve", lambda e, s=s, kc=kc, is_s=is_s: e.scalar_tensor_tensor(
                            out=tmpb[s][:], in0=xt[:, kc, :], scalar=modcol(jscale, kc, is_s), in1=rstd[:],
                            op0=ALU.mult, op1=ALU.mult), reads=[xr[g], rr, modres], writes=[tmr[s]])
                        K.op("pool", lambda e, s=s, kc=kc, is_s=is_s, hb=hb, i=i: e.tensor_scalar(
                            out=ho[hb][:, i, :], in0=tmpb[s][:], scalar1=modcol(jshift, kc, is_s), scalar2=None,
                            op0=ALU.add), reads=[tmr[s], modres], writes=[hr[hb]])
                    K.dma("sp", hTv[:, g * 8:(g + 1) * 8, c0:c0 + 512], ho[hb][:], reads=[hr[hb]])
            K.barrier()

    def load_act_tile(at, ares, srcT, nk, c0, ncols):
        v = srcT.rearrange("(k p) t -> p k t", p=128)
        step = 8
        for k0 in range(0, nk, step):
            k1 = min(nk, k0 + step)
            K.dma("sp", at[:, k0:k1, 0:ncols], v[:, k0:k1, c0:c0 + ncols], writes=[ares], disjoint=(k0 != 0))

    def phase_gemm(srcT, nk, wmat, blocks, epilogue, name):
        with ExitStack() as ph:
            walloc(ph)
            at = sb(name + "_act", [128, nk, 1024], BF16, ph)
            ares = Res()
            for tile in cfg.tiles:
                (c0, ncols, is_s) = tile
                NT = ncols // 512
                load_act_tile(at, ares, srcT, nk, c0, ncols)
                for (bc0, bw) in blocks:
                    wis = []
                    for k0 in range(0, nk, 32):
                        wi = wnext()
                        wload(wi, wmat, k0, min(32, nk - k0), bc0, bw, view_cols=256)
                        wis.append(wi)
                    for m0 in range(0, bw, 128):
                        mw = min(128, bw - m0)
                        pbs = [bank() for _ in range(NT)]
                        for kc in range(nk):
                            wi = wis[kc // 32]
                            wv = wview(wi, 32, 256)
                            for nt in range(NT):
                                K.op("pe", lambda e, wv=wv, kc=kc, m0=m0, mw=mw, nt=nt, pb=pbs[nt]: e.matmul(
                                    psum[pb][0:mw, :], wv[:, kc % 32, m0:m0 + mw], at[:, kc, nt * 512:(nt + 1) * 512],
                                    start=(kc == 0), stop=(kc == nk - 1)),
                                    reads=[wres[wi], ares], writes=[psr[pbs[nt]]])
                        for nt in range(NT):
                            epilogue(ph, pbs[nt], bc0 + m0, mw, c0 + nt * 512, is_s)
            K.barrier()

    class Ring:
        def __init__(self, ph, name, n, shape, dt):
            self.t = [sb("%s%d" % (name, i), shape, dt, ph) for i in range(n)]
            self.r = [Res() for _ in range(n)]
            self.i = 0

        def next(self):
            j = self.i % len(self.t)
            self.i += 1
            return self.t[j], self.r[j]

    def phase_proj(l):
        st = {}

        def epi(ph, pb, row0, mw, col0, is_s):
            if "ring" not in st:
                st["ring"] = Ring(ph, "pj_o", 4, [128, 512], F32)
                st["n"] = 0
            t, r = st["ring"].next()
            st["n"] += 1
            if st["n"] % 2:
                K.op("act", lambda e: e.activation(out=t[0:mw, :], in_=psum[pb][0:mw, :], func=AF.Copy),
                     reads=[psr[pb]], writes=[r])
            else:
                K.op("dve", lambda e: e.tensor_copy(out=t[0:mw, :], in_=psum[pb][0:mw, :]), reads=[psr[pb]], writes=[r])
            K.dma("sp", projT[row0:row0 + mw, col0:col0 + 512], t[0:mw, :], reads=[r])

        blocks = [(c, 256) for c in range(0, 3072, 256)] + [(3072, 16)] + [(c, 256) for c in range(3088, N_IN, 256)]
        phase_gemm(hT, KC, w_in[l], blocks, epi, "pj")

    def phase_resid(l, srcT, nk, wmat, jgate, name):
        st = {}

        def epi(ph, pb, row0, mw, col0, is_s):
            if "xi" not in st:
                st["xi"] = Ring(ph, name + "_xi", 4, [128, 512], F32)
                st["xo"] = Ring(ph, name + "_xo", 4, [128, 512], F32)
            ti, ri = st["xi"].next()
            to, ro = st["xo"].next()
            kc = row0 // 128
            K.dma("sp", ti[:], xT[row0:row0 + 128, col0:col0 + 512], writes=[ri])
            K.op("dve", lambda e: e.scalar_tensor_tensor(
                out=to[:], in0=psum[pb][:, :], scalar=modcol(jgate, kc, is_s), in1=ti[:], op0=ALU.mult, op1=ALU.add),
                reads=[psr[pb], ri, modres], writes=[ro])
            K.dma("sp", xT[row0:row0 + 128, col0:col0 + 512], to[:], reads=[ro])

        blocks = [(c, 256) for c in range(0, D, 256)]
        phase_gemm(srcT, nk, wmat, blocks, epi, name)

    def phase_gateup(l):
        with ExitStack() as ph:
            walloc(ph)
            at = sb("gu_act", [128, KC, 1024], BF16, ph)
            ares = Res()
            sg = Ring(ph, "gu_sg", 3, [128, 512], F32)
            ho = Ring(ph, "gu_ho", 3, [128, 512], BF16)
            for (c0, ncols, is_s) in cfg.tiles:
                NT = ncols // 512
                load_act_tile(at, ares, hT, KC, c0, ncols)
                for bc0 in range(0, FFN, 256):
                    wg = wnext()
                    wload(wg, w_gate[l], 0, KC, bc0, 256)
                    wu = wnext()
                    wload(wu, w_up[l], 0, KC, bc0, 256)
                    for m in range(2):
                        pg = [bank() for _ in range(NT)]
                        pu = [bank() for _ in range(NT)]
                        for (wi, pbs) in ((wg, pg), (wu, pu)):
                            wv = wview(wi, 32, 256)
                            for kc in range(KC):
                                for nt in range(NT):
                                    K.op("pe", lambda e, wv=wv, kc=kc, m=m, nt=nt, pb=pbs[nt]: e.matmul(
                                        psum[pb][:, :], wv[:, kc, m * 128:(m + 1) * 128],
                                        at[:, kc, nt * 512:(nt + 1) * 512], start=(kc == 0), stop=(kc == KC - 1)),
                                        reads=[wres[wi], ares], writes=[psr[pbs[nt]]])
                        for nt in range(NT):
                            ts, rs = sg.next()
                            th, rh = ho.next()
                            K.op("act", lambda e, ts=ts, pb=pg[nt]: e.activation(out=ts[:], in_=psum[pb][:, :], func=AF.Silu),
                                 reads=[psr[pg[nt]]], writes=[rs])
                            K.op("dve", lambda e, ts=ts, th=th, pb=pu[nt]: e.tensor_tensor(
                                out=th[:], in0=ts[:], in1=psum[pb][:, :], op=ALU.mult),
                                reads=[rs, psr[pu[nt]]], writes=[rh])
                            r0 = bc0 + m * 128
                            K.dma("sp", hidT[r0:r0 + 128, c0 + nt * 512:c0 + (nt + 1) * 512], th[:], reads=[rh])
            K.barrier()

    def phase_transpose_out():
        with ExitStack() as ph:
            xt = sb("to_x", [128, KC, 512], F32, ph)
            xr = Res()
            yo = [sb("to_y%d" % i, [128, D], F32, ph) for i in range(2)]
            yr = [Res() for _ in range(2)]
            xTv = xT.rearrange("(k p) t -> p k t", p=128)
            blk = 0
            for (c0, ncols, is_s) in cfg.sub:
                for g in range(4):
                    K.dma("sp", xt[:, g * 8:(g + 1) * 8, :], xTv[:, g * 8:(g + 1) * 8, c0:c0 + 512], writes=[xr],
                          disjoint=(g != 0))
                for j in range(4):
                    yb = blk % 2
                    blk += 1
                    for g in range(8):
                        b = bank()
                        for i in range(4):
                            kc = g * 4 + i
                            K.op("pe", lambda e, b=b, i=i, kc=kc, j=j: e.transpose(
                                psum[b][:, i * 128:(i + 1) * 128], xt[:, kc, j * 128:(j + 1) * 128], ident[:]),
                                reads=[xr, cres], writes=[psr[b]])
                        dst = yo[yb][:, g * 512:(g + 1) * 512]
                        if g % 2 == 0:
                            K.op("act", lambda e, dst=dst, b=b: e.activation(out=dst, in_=psum[b][:, :], func=AF.Copy),
                                 reads=[psr[b]], writes=[yr[yb]])
                        else:
                            K.op("dve", lambda e, dst=dst, b=b: e.tensor_copy(out=dst, in_=psum[b][:, :]),
                                 reads=[psr[b]], writes=[yr[yb]])
                    tok0 = c0 + j * 128
                    dstd = ys[tok0:tok0 + 128, :] if is_s else yp[tok0 - LS:tok0 - LS + 128, :]
                    K.dma("sp", dstd, yo[yb][:], reads=[yr[yb]])
            K.barrier()

    lnres = Res()

    def rstd_from_ps(pb, lnt, out, rres, epsc, n=512):
        K.op("act", lambda e: e.activation(out=lnt[:, 0:n], in_=psum[pb][:, 0:n], func=AF.Ln, bias=epsc[:]),
             reads=[psr[pb]], writes=[lnres])
        K.op("act", lambda e: e.activation(out=out, in_=lnt[:, 0:n], func=AF.Exp, scale=-0.5),
             reads=[lnres], writes=[rres])

    def phase_gmlp(l):
        with ExitStack() as ph:
            wsT = sb("g_wsT", [128, 4, 128], F32, ph)
            bsr = sb("g_bsr", [1, 512], F32, ph)
            ones1 = sb("g_ones1", [1, 128], F32, ph)
            pvt = sb("g_pv", [128, 8], F32, ph)
            onesF = sb("g_onesF", [128, 128], F32, ph)
            epsc = sb("g_eps", [128, 1], F32, ph)
            cr = Res()
            K.dma("sp", wsT[:], a_wsT[l], writes=[cr])
            K.dma("sp", bsr[:], bs_row[l], writes=[cr], disjoint=True)
            K.dma("sp", pvt[:], pv[l], writes=[cr], disjoint=True)
            K.op("pool", lambda e: e.memset(ones1[:], 1.0), writes=[cr])
            K.op("pool", lambda e: e.memset(onesF[:], 1.0 / 512), writes=[cr])
            K.op("pool", lambda e: e.memset(epsc[:], EPS), writes=[cr])
            u = [sb("g_u%d" % i, [128, 4, 512], F32, ph) for i in range(2)]
            v = [sb("g_v%d" % i, [128, 4, 512], F32, ph) for i in range(2)]
            ur = [Res() for _ in range(2)]
            vr = [Res() for _ in range(2)]
            sqv = sb("g_sq", [128, 4, 512], F32, ph)
            sqr = Res()
            lnt = sb("g_ln", [128, 512], F32, ph)
            rstd = sb("g_rstd", [128, 512], F32, ph)
            rr = Res()
            vn = sb("g_vn", [128, 4, 512], F32, ph)
            vnr = Res()
            vtok = Ring(ph, "g_vtok", 2, [128, 4, 128], F32)
            yo = Ring(ph, "g_yo", 2, [128, 4, 512], BF16)
            yaTv = yaT.rearrange("(k p) t -> p k t", p=128)
            for si, (c0, ncols, is_s) in enumerate(cfg.sub):
                i2 = si % 2
                K.dma("sp", u[i2][:], pTc(0, 4, c0, c0 + 512), writes=[ur[i2]])
                K.dma("sp", v[i2][:], pTc(4, 8, c0, c0 + 512), writes=[vr[i2]])
                K.op("act", lambda e, i2=i2: e.activation(out=u[i2][:], in_=u[i2][:], func=AF.Gelu_apprx_tanh),
                     reads=[ur[i2]], writes=[ur[i2]])
                K.op("act", lambda e, i2=i2: e.activation(out=v[i2][:], in_=v[i2][:], func=AF.Gelu_apprx_tanh),
                     reads=[vr[i2]], writes=[vr[i2]])
                K.op("pool", lambda e, i2=i2: e.tensor_tensor(out=sqv[:], in0=v[i2][:], in1=v[i2][:], op=ALU.mult),
                     reads=[vr[i2]], writes=[sqr])
                b = bank()
                for g in range(4):
                    K.op("pe", lambda e, g=g, b=b: e.matmul(psum[b][:, :], onesF[:], sqv[:, g, :],
                                                            start=(g == 0), stop=(g == 3)),
                         reads=[sqr, cr], writes=[psr[b]])
                rstd_from_ps(b, lnt, rstd[:], rr, epsc)
                for g in range(4):
                    K.op("dve", lambda e, g=g, i2=i2: e.scalar_tensor_tensor(
                        out=vn[:, g, :], in0=v[i2][:, g, :], scalar=pvt[:, g:g + 1], in1=rstd[:],
                        op0=ALU.mult, op1=ALU.mult), reads=[vr[i2], rr, cr], writes=[vnr])
                yt, yr = yo.next()
                for g in range(4):
                    bt = bank()
                    for j in range(4):
                        K.op("pe", lambda e, g=g, j=j, bt=bt: e.transpose(
                            psum[bt][:, j * 128:(j + 1) * 128], vn[:, g, j * 128:(j + 1) * 128], ident[:]),
                            reads=[vnr, cres], writes=[psr[bt]])
                    vt, vtr = vtok.next()
                    K.op("act", lambda e, vt=vt, bt=bt: e.activation(
                        out=vt[:], in_=psum[bt][:, :].rearrange("p (j c) -> p j c", c=128), func=AF.Copy),
                        reads=[psr[bt]], writes=[vtr])
                    bm = bank()
                    for j in range(4):
                        K.op("pe", lambda e, g=g, j=j, bm=bm, vt=vt: e.matmul(
                            psum[bm][:, j * 128:(j + 1) * 128], vt[:, j, :], wsT[:, g, :], start=True, stop=False),
                            reads=[vtr, cr], writes=[psr[bm]])
                        K.op("pe", lambda e, g=g, j=j, bm=bm: e.matmul(
                            psum[bm][:, j * 128:(j + 1) * 128], ones1[0:1, :], bsr[0:1, g * 128:(g + 1) * 128],
                            start=False, stop=True), reads=[cr], writes=[psr[bm]])
                    K.op("dve", lambda e, g=g, bm=bm, yt=yt, i2=i2: e.tensor_tensor(
                        out=yt[:, g, :], in0=u[i2][:, g, :], in1=psum[bm][:, :], op=ALU.mult),
                        reads=[ur[i2], psr[bm]], writes=[yr])
                K.dma("sp", yaTv[:, 0:4, c0:c0 + 512], yt[:], reads=[yr])
            K.barrier()

    SCALE = 128.0 ** -0.5

    def phase_attn(l):
        with ExitStack() as ph:
            pvt = sb("a_pv", [128, 8], F32, ph)
            onesH = sb("a_onesH", [128, 128], F32, ph)
            onesb = sb("a_onesb", [128, 128], BF16, ph)
            permt = sb("a_perm", [128, 128], F32, ph)
            epsc = sb("a_eps", [128, 1], F32, ph)
            cr = Res()
            K.dma("sp", pvt[:], pv[l], writes=[cr])
            K.dma("sp", permt[:], perm_d[:, :], writes=[cr], disjoint=True)
            K.op("pool", lambda e: e.memset(onesH[:], 1.0 / 128), writes=[cr])
            K.op("pool", lambda e: e.memset(onesb[:], 1.0), writes=[cr])
            K.op("pool", lambda e: e.memset(epsc[:], EPS), writes=[cr])
            NBS = 2 + LS // 128
            KT = sb("a_KT", [128, 2, 256 + LS], BF16, ph)
            Vt = sb("a_Vt", [128, NBS, 256], BF16, ph)
            KTp = sb("a_KTp", [128, 2, cfg.LPT], BF16, ph)
            Vp = sb("a_Vp", [128, cfg.LPT // 128, 256], BF16, ph)
            kvres = Res()
            with ExitStack() as p1:
                ckt = sb("a_ck", [128, 2, 256], F32, p1)
                cvt = sb("a_cv", [128, 2, 256], F32, p1)
                ckr = Res()
                K.dma("sp", ckt[:], ck[l].rearrange("(b p) f -> p b f", p=128), writes=[ckr])
                K.dma("sp", cvt[:], cv[l].rearrange("(b p) f -> p b f", p=128), writes=[ckr], disjoint=True)
                K.op("dve", lambda e: e.tensor_copy(out=Vt[:, 0:2, :], in_=cvt[:]), reads=[ckr], writes=[kvres])
                for kv in range(2):
                    b = bank()
                    for blk in range(2):
                        K.op("pe", lambda e, b=b, blk=blk, kv=kv: e.transpose(
                            psum[b][:, blk * 128:(blk + 1) * 128], ckt[:, blk, kv * 128:(kv + 1) * 128], ident[:]),
                            reads=[ckr, cres], writes=[psr[b]])
                    K.op("dve", lambda e, b=b, kv=kv: e.tensor_copy(out=KT[:, kv, 0:256], in_=psum[b][:, 0:256]),
                         reads=[psr[b]], writes=[kvres])
                aq = [sb("a_aq%d" % i, [128, 12, 512], F32, p1) for i in range(2)]
                aqr = [Res() for _ in range(2)]
                cs = [sb("a_cs%d" % i, [128, 2, 512], F32, p1) for i in range(2)]
                csr = [Res() for _ in range(2)]
                sq = Ring(p1, "a_sq", 2, [128, 512], F32)
                lnt = sb("a_ln", [128, 512], F32, p1)
                rs = Ring(p1, "a_rs", 2, [128, 512], F32)
                xn = Ring(p1, "a_xn", 2, [128, 512], F32)
                r1 = Ring(p1, "a_r1", 2, [128, 512], F32)
                r2 = Ring(p1, "a_r2", 2, [128, 512], F32)
                qo = Ring(p1, "a_qo", 2, [128, 8, 512], BF16)
                kst = Ring(p1, "a_kst", 2, [128, 2, 256], F32)
                vst = Ring(p1, "a_vst", 2, [128, 2, 256], F32)
                qnTv = qnT.rearrange("(k p) t -> p k t", p=128)
                for si, (c0, ncols, is_s) in enumerate(cfg.sub):
                    i2 = si % 2
                    K.dma("sp", aq[i2][:], pTv_off(3088, 12, c0), writes=[aqr[i2]])
                    if is_s:
                        K.dma("sp", cs[i2][:, 0, :], ropeC[:, c0:c0 + 512], writes=[csr[i2]])
                        K.dma("sp", cs[i2][:, 1, :], ropeS[:, c0:c0 + 512], writes=[csr[i2]], disjoint=True)
                    qt_, qr_ = qo.next()
                    if not is_s:
                        kf = sb("a_kf%d" % si, [128, 2, 512], F32, p1)
                        kfr = Res()
                    for c in range(10):
                        st_, sr_ = sq.next()
                        K.op("pool", lambda e, c=c, i2=i2, st_=st_: e.tensor_tensor(
                            out=st_[:], in0=aq[i2][:, c, :], in1=aq[i2][:, c, :], op=ALU.mult),
                            reads=[aqr[i2]], writes=[sr_])
                        b = bank()
                        K.op("pe", lambda e, b=b, st_=st_: e.matmul(psum[b][:, :], onesH[:], st_[:], start=True, stop=True),
                             reads=[sr_, cr], writes=[psr[b]])
                        rt_, rr_ = rs.next()
                        rstd_from_ps(b, lnt, rt_[:], rr_, epsc)
                        gcol = 5 if c < 8 else 6
                        if is_s:
                            xt_, xr_ = xn.next()
                            K.op("dve", lambda e, c=c, i2=i2, xt_=xt_, rt_=rt_, gcol=gcol: e.scalar_tensor_tensor(
                                out=xt_[:], in0=aq[i2][:, c, :], scalar=pvt[:, gcol:gcol + 1], in1=rt_[:],
                                op0=ALU.mult, op1=ALU.mult), reads=[aqr[i2], rr_, cr], writes=[xr_])
                            b2 = bank()
                            K.op("pe", lambda e, b2=b2, xt_=xt_: e.matmul(psum[b2][:, :], permt[:], xt_[:], start=True, stop=True),
                                 reads=[xr_, cr], writes=[psr[b2]])
                            r1t, r1r = r1.next()
                            r2t, r2r = r2.next()
                            K.op("pool", lambda e, xt_=xt_, r1t=r1t, i2=i2: e.tensor_tensor(
                                out=r1t[:], in0=xt_[:], in1=cs[i2][:, 0, :], op=ALU.mult),
                                reads=[xr_, csr[i2]], writes=[r1r])
                            K.op("dve", lambda e, b2=b2, r2t=r2t, i2=i2: e.tensor_tensor(
                                out=r2t[:], in0=psum[b2][:, :], in1=cs[i2][:, 1, :], op=ALU.mult),
                                reads=[psr[b2], csr[i2]], writes=[r2r])
                            if c < 8:
                                K.op("pool", lambda e, r1t=r1t, r2t=r2t, qt_=qt_, c=c: e.tensor_tensor(
                                    out=qt_[:, c, :], in0=r1t[:], in1=r2t[:], op=ALU.add),
                                    reads=[r1r, r2r], writes=[qr_])
                            else:
                                K.op("pool", lambda e, r1t=r1t, r2t=r2t, c=c, c0=c0: e.tensor_tensor(
                                    out=KT[:, c - 8, 256 + c0:256 + c0 + 512], in0=r1t[:], in1=r2t[:], op=ALU.add),
                                    reads=[r1r, r2r], writes=[kvres])
                        else:
                            if c < 8:
                                K.op("dve", lambda e, c=c, i2=i2, rt_=rt_, gcol=gcol, qt_=qt_: e.scalar_tensor_tensor(
                                    out=qt_[:, c, :], in0=aq[i2][:, c, :], scalar=pvt[:, gcol:gcol + 1], in1=rt_[:],
                                    op0=ALU.mult, op1=ALU.mult), reads=[aqr[i2], rr_, cr], writes=[qr_])
                            else:
                                K.op("dve", lambda e, c=c, i2=i2, rt_=rt_, gcol=gcol, kf=kf: e.scalar_tensor_tensor(
                                    out=kf[:, c - 8, :], in0=aq[i2][:, c, :], scalar=pvt[:, gcol:gcol + 1], in1=rt_[:],
                                    op0=ALU.mult, op1=ALU.mult), reads=[aqr[i2], rr_, cr], writes=[kfr])
                                K.op("pool", lambda e, c=c, kf=kf, c0=c0: e.tensor_copy(
                                    out=KTp[:, c - 8, c0 - LS:c0 - LS + 512], in_=kf[:, c - 8, :]),
                                    reads=[kfr], writes=[kvres])
                    K.dma("sp", qnTv[:, 0:8, c0:c0 + 512], qt_[:], reads=[qr_])
                    for j in range(4):
                        bv = bank()
                        for kv in range(2):
                            K.op("pe", lambda e, bv=bv, kv=kv, j=j, i2=i2: e.transpose(
                                psum[bv][:, kv * 128:(kv + 1) * 128], aq[i2][:, 10 + kv, j * 128:(j + 1) * 128], ident[:]),
                                reads=[aqr[i2], cres], writes=[psr[bv]])
                        if is_s:
                            blk = 2 + c0 // 128 + j
                            K.op("act", lambda e, bv=bv, blk=blk: e.activation(out=Vt[:, blk, :], in_=psum[bv][:, 0:256], func=AF.Copy),
                                 reads=[psr[bv]], writes=[kvres])
                        else:
                            tokp = c0 - LS + j * 128
                            pi, t0 = tokp // LP, tokp % LP
                            vt_, vr_ = vst.next()
                            K.op("act", lambda e, bv=bv, vt_=vt_: e.activation(out=vt_[:, 0, :], in_=psum[bv][:, 0:256], func=AF.Copy),
                                 reads=[psr[bv]], writes=[vr_])
                            K.op("pool", lambda e, vt_=vt_, tokp=tokp: e.tensor_copy(out=Vp[:, tokp // 128, :], in_=vt_[:, 0, :]),
                                 reads=[vr_], writes=[kvres])
                            K.dma("sp", new_v[pi, l, t0:t0 + 128, :], vt_[:, 0, :], reads=[vr_])
                            bk = bank()
                            for kv in range(2):
                                K.op("pe", lambda e, bk=bk, kv=kv, j=j, kf=kf: e.transpose(
                                    psum[bk][:, kv * 128:(kv + 1) * 128], kf[:, kv, j * 128:(j + 1) * 128], ident[:]),
                                    reads=[kfr, cres], writes=[psr[bk]])
                            kt_, kr_ = kst.next()
                            K.op("dve", lambda e, bk=bk, kt_=kt_: e.tensor_copy(out=kt_[:, 0, :], in_=psum[bk][:, 0:256]),
                                 reads=[psr[bk]], writes=[kr_])
                            K.dma("sp", new_k[pi, l, t0:t0 + 128, :], kt_[:, 0, :], reads=[kr_])
                K.barrier()
            with ExitStack() as p2:
                qn = [sb("a_qn%d" % i, [128, 8, 512], BF16, p2) for i in range(2)]
                qnr = [Res() for _ in range(2)]
                pt = Ring(p2, "a_pt", 3, [128, 512], BF16)
                rv = Ring(p2, "a_rv", 2, [128, 512], F32)
                yo = Ring(p2, "a_yo", 2, [128, 512], BF16)
                cnt = {"a": 0, "s": 0}

                def attend(qap, nq, kaps, vaps, dst):
                    ia = cnt["a"]
                    cnt["a"] += 1
                    po, pl = ia % 2, 2 + ia % 2
                    n = len(kaps)
                    sbk = {}

                    def S(kb):
                        b = 4 + cnt["s"] % 4
                        cnt["s"] += 1
                        sbk[kb] = b
                        K.op("pe", lambda e: e.matmul(psum[b][:, 0:nq], kaps[kb], qap, start=True, stop=True),
                             reads=[kvres, qres_cur[0]], writes=[psr[b]])
                    S(0)
                    for kb in range(n):
                        if kb + 1 < n:
                            S(kb + 1)
                        b = sbk[kb]
                        pt_, pr_ = pt.next()
                        K.op("act", lambda e, b=b, pt_=pt_: e.activation(out=pt_[:, 0:nq], in_=psum[b][:, 0:nq], func=AF.Exp, scale=SCALE),
                             reads=[psr[b]], writes=[pr_])
                        K.op("pe", lambda e, kb=kb, pt_=pt_: e.matmul(psum[po][:, 0:nq], vaps[kb], pt_[:, 0:nq],
                                                                    start=(kb == 0), stop=(kb == n - 1)),
                             reads=[kvres, pr_], writes=[psr[po]])
                        K.op("pe", lambda e, kb=kb, pt_=pt_: e.matmul(psum[pl][:, 0:nq], onesb[:], pt_[:, 0:nq],
                                                                    start=(kb == 0), stop=(kb == n - 1)),
                             reads=[cr, pr_], writes=[psr[pl]])
                    rv_, rvr = rv.next()
                    yo_, yor = yo.next()
                    K.op("dve", lambda e: e.reciprocal(out=rv_[:, 0:nq], in_=psum[pl][:, 0:nq]), reads=[psr[pl]], writes=[rvr])
                    K.op("dve", lambda e: e.tensor_tensor(out=yo_[:, 0:nq], in0=psum[po][:, 0:nq], in1=rv_[:, 0:nq], op=ALU.mult),
                         reads=[psr[po], rvr], writes=[yor])
                    K.dma("sp", dst, yo_[:, 0:nq], reads=[yor])

                qres_cur = [None]
                for si, (c0, ncols, is_s) in enumerate(cfg.sub):
                    i2 = si % 2
                    K.dma("sp", qn[i2][:], qnT.rearrange("(k p) t -> p k t", p=128)[:, 0:8, c0:c0 + 512], writes=[qnr[i2]])
                    qres_cur[0] = qnr[i2]
                    for h in range(8):
                        kvh = h // 4
                        if is_s:
                            kaps = [KT[:, kvh, kb * 128:(kb + 1) * 128] for kb in range(NBS)]
                            vaps = [Vt[:, kb, kvh * 128:(kvh + 1) * 128] for kb in range(NBS)]
                            attend(qn[i2][:, h, :], 512, kaps, vaps, ycT[h * 128:(h + 1) * 128, c0:c0 + 512])
                        else:
                            for pi in range(512 // LP):
                                t0 = c0 - LS + pi * LP
                                kaps = [KTp[:, kvh, t0 + kb * 128:t0 + (kb + 1) * 128] for kb in range(2)]
                                vaps = [Vp[:, t0 // 128 + kb, kvh * 128:(kvh + 1) * 128] for kb in range(2)]
                                attend(qn[i2][:, h, pi * LP:(pi + 1) * LP], LP, kaps, vaps,
                                       ycT[h * 128:(h + 1) * 128, c0 + pi * LP:c0 + (pi + 1) * LP])
                K.barrier()

    def pTc(ch0, ch1, a, b):
        return projT[(ch0) * 128:(ch1) * 128, a:b].rearrange("(k p) t -> p k t", p=128)

    def pTv_off(row0, nch, c0):
        return projT[row0:row0 + nch * 128, c0:c0 + 512].rearrange("(k p) t -> p k t", p=128)

    def phase_merge(l):
        with ExitStack() as ph:
            walloc(ph)
            at = sb("mg_act", [128, 18, 1024], BF16, ph)
            ares = Res()
            glf = sb("mg_glf", [128, 2, 1024], F32, ph)
            glr = Res()
            sig = Ring(ph, "mg_sig", 3, [128, 512], F32)
            prod = Ring(ph, "mg_prod", 6, [128, 512], F32)
            acc = Ring(ph, "mg_acc", 2, [128, 512], F32)
            mo = Ring(ph, "mg_out", 3, [128, 512], BF16)
            srcs = ((yaT, 4, 2), (ybT, 4, 6), (ycT, 8, 10))
            wsl = ((w_br_a, 4, 6), (w_br_b, 4, 10), (w_br_c, 8, 14))
            for (c0, ncols, is_s) in cfg.tiles:
                NT = ncols // 512
                K.dma("sp", glf[:, :, 0:ncols], projT[4624:4880, c0:c0 + ncols].rearrange("(k p) t -> p k t", p=128),
                      writes=[glr])
                K.op("dve", lambda e, ncols=ncols: e.tensor_copy(out=at[:, 0:2, 0:ncols], in_=glf[:, :, 0:ncols]),
                     reads=[glr], writes=[ares])
                for (src, nch, a0) in srcs:
                    K.dma("sp", at[:, a0:a0 + nch, 0:ncols], src.rearrange("(k p) t -> p k t", p=128)[:, :, c0:c0 + ncols],
                          writes=[ares], disjoint=True)
                for bc0 in range(0, D, 256):
                    wi = wnext()
                    for b in range(3):
                        wload(wi, w_mg[l], 0, 2, b * D + bc0, 256, slot0=2 * b, view_cols=256)
                    for (wm, nch, s0) in wsl:
                        wload(wi, wm[l], 0, nch, bc0, 256, slot0=s0, view_cols=256)
                    wv = wview(wi, 32, 256)
                    for m in range(2):
                        for nt in range(NT):
                            prods = []
                            for b in range(3):
                                pg = bank()
                                pb = bank()
                                for k in range(2):
                                    K.op("pe", lambda e, pg=pg, b=b, k=k, m=m, nt=nt: e.matmul(
                                        psum[pg][:, :], wv[:, 2 * b + k, m * 128:(m + 1) * 128],
                                        at[:, k, nt * 512:(nt + 1) * 512], start=(k == 0), stop=(k == 1)),
                                        reads=[wres[wi], ares], writes=[psr[pg]])
                                nch, s0, a0 = wsl[b][1], wsl[b][2], srcs[b][2]
                                for k in range(nch):
                                    K.op("pe", lambda e, pb=pb, k=k, m=m, nt=nt, s0=s0, a0=a0, nch=nch: e.matmul(
                                        psum[pb][:, :], wv[:, s0 + k, m * 128:(m + 1) * 128],
                                        at[:, a0 + k, nt * 512:(nt + 1) * 512], start=(k == 0), stop=(k == nch - 1)),
                                        reads=[wres[wi], ares], writes=[psr[pb]])
                                sg_, sgr = sig.next()
                                pd_, pdr = prod.next()
                                K.op("act", lambda e, pg=pg, sg_=sg_: e.activation(out=sg_[:], in_=psum[pg][:, :], func=AF.Sigmoid),
                                     reads=[psr[pg]], writes=[sgr])
                                K.op("dve", lambda e, pb=pb, sg_=sg_, pd_=pd_: e.tensor_tensor(
                                    out=pd_[:], in0=sg_[:], in1=psum[pb][:, :], op=ALU.mult),
                                    reads=[sgr, psr[pb]], writes=[pdr])
                                prods.append((pd_, pdr))
                            ac_, acr = acc.next()
                            mo_, mor = mo.next()
                            K.op("pool", lambda e, ac_=ac_, prods=prods: e.tensor_tensor(
                                out=ac_[:], in0=prods[0][0][:], in1=prods[1][0][:], op=ALU.add),
                                reads=[prods[0][1], prods[1][1]], writes=[acr])
                            K.op("pool", lambda e, ac_=ac_, mo_=mo_, prods=prods: e.tensor_tensor(
                                out=mo_[:], in0=ac_[:], in1=prods[2][0][:], op=ALU.add),
                                reads=[acr, prods[2][1]], writes=[mor])
                            r0 = bc0 + m * 128
                            K.dma("sp", mrgT[r0:r0 + 128, c0 + nt * 512:c0 + (nt + 1) * 512], mo_[:], reads=[mor])
            K.barrier()

    def phase_dnet(l):
        seqs = []
        if LS:
            seqs.append((0, LS, True, 0))
        for pi in range(cfg.NP):
            seqs.append((LS + pi * LP, LP, False, pi))
        import os as _osc
        CUT = int(_osc.environ.get("D1_CUT", "0"))

        class StopEmit(Exception):
            pass

        def cut(n):
            if CUT == n:
                raise StopEmit()
        cutflag = [False]
        with ExitStack() as ph:
            try:
                mk = sb("d_mask", [128, 7, 512], F32, ph)
                cw = sb("d_cw", [128, 12, 5], F32, ph)
                abp = sb("d_abp", [16, 2], F32, ph)
                nexpA = sb("d_nexpA", [16, 1], F32, ph)
                one16 = sb("d_one16", [128, 1], F32, ph)
                epsc = sb("d_eps", [128, 1], F32, ph)
                cr = Res()
                K.dma("sp", mk[:], dmask.rearrange("s p f -> p s f"), writes=[cr])
                K.dma("sp", cw[:], b_convT[l], writes=[cr], disjoint=True)
                K.dma("sp", abp[:], ab_par[l], writes=[cr], disjoint=True)
                K.op("pool", lambda e: e.memset(one16[:], 1.0), writes=[cr])
                K.op("pool", lambda e: e.memset(epsc[:], EPS), writes=[cr])
                K.op("act", lambda e: e.activation(out=nexpA[:], in_=abp[:, 0:1], func=AF.Exp), reads=[cr], writes=[cr])
                K.op("dve", lambda e: e.tensor_scalar(out=nexpA[:], in0=nexpA[:], scalar1=-1.0, scalar2=None, op0=ALU.mult),
                     reads=[cr], writes=[cr])
                Mdir = (mk[:, 0, 0:128], mk[:, 0, 128:256])
                maskA = (mk[:, 1, :], mk[:, 2, :])
                maskB = (mk[:, 3, :], mk[:, 4, :])
                ident4 = mk[:, 5, :]
                onesF = mk[:, 6, 0:128]
                cvt = sb("d_cv", [128, 12, 512], F32, ph)
                cvrs = [Res() for _ in range(12)]
                xin = Ring(ph, "d_xin", 2, [128, 4, 516], F32)
                abt = sb("d_abt", [16, 512], F32, ph)
                e1 = sb("d_e1", [16, 512], F32, ph)
                gT = sb("d_gT", [16, 512], F32, ph)
                bT = sb("d_bT", [16, 512], F32, ph)
                gres = Res()
                lnt = sb("d_ln", [128, 512], F32, ph)
                T = Ring(ph, "d_T", 8, [128, 512], F32)
                DED = [{n: (sb("d_%s%d" % (n, dd), [128, 512], F32, ph), Res()) for n in ("x", "bv", "bgk", "p0", "p1", "pt0", "pt1")}
                       for dd in range(2)]
                OUT = Ring(ph, "d_OUT", 2, [128, 5, 512], F32)
                sm = Ring(ph, "d_sm", 12, [128, 32], F32)
                ktok = Ring(ph, "d_ktok", 2, [128, 512], F32)
                vtok = Ring(ph, "d_vtok", 2, [128, 512], F32)
                kks = Ring(ph, "d_kk", 2, [128, 512], F32)
                qks = Ring(ph, "d_qk", 2, [128, 512], F32)
                gtk = Ring(ph, "d_gtk", 2, [128, 32], F32)
                ncv = [0]

                def copy_ps(dst, b, wres_, rd=()):
                    ncv[0] += 1
                    if ncv[0] % 2:
                        K.op("act", lambda e: e.activation(out=dst, in_=psum[b][:, :], func=AF.Copy), reads=[psr[b]] + list(rd), writes=[wres_])
                    else:
                        K.op("dve", lambda e: e.tensor_copy(out=dst, in_=psum[b][:, :]), reads=[psr[b]] + list(rd), writes=[wres_])

                for si, (c0, ncols, is_s) in enumerate(cfg.sub):
                    if is_s:
                        segs = [(0, 512, 0, LS)]
                    else:
                        segs = [(j * LP, LP, c0 + j * LP, c0 + (j + 1) * LP) for j in range(512 // LP)]
                    nop = 0
                    for (s0, slen, lo, hi) in segs:
                        a = c0 + s0
                        va, vb = max(lo, a - 2), min(hi, a + slen + 2)
                        for grp in range(3):
                            xt_, xr_ = xin.next()
                            if va > a - 2 or vb < a + slen + 2:
                                K.op("pool", lambda e, xt_=xt_: e.memset(xt_[:], 0.0), writes=[xr_])
                            K.dma("sp", xt_[:, :, va - (a - 2):vb - (a - 2)], pTc(8 + grp * 4, 12 + grp * 4, va, vb), writes=[xr_])
                            for cc in range(4):
                                c = grp * 4 + cc
                                eng = "dve"
                                nop += 1
                                dst = cvt[:, c, s0:s0 + slen]
                                K.op(eng, lambda e, dst=dst, xt_=xt_, cc=cc, c=c, slen=slen: e.tensor_scalar(
                                    out=dst, in0=xt_[:, cc, 0:slen], scalar1=cw[:, c, 0:1], scalar2=None, op0=ALU.mult),
                                    reads=[xr_, cr], writes=[cvrs[c]])
                                for i in range(1, 5):
                                    K.op(eng, lambda e, dst=dst, xt_=xt_, cc=cc, c=c, i=i, slen=slen: e.scalar_tensor_tensor(
                                        out=dst, in0=xt_[:, cc, i:i + slen], scalar=cw[:, c, i:i + 1], in1=dst,
                                        op0=ALU.mult, op1=ALU.add), reads=[xr_, cr, cvrs[c]], writes=[cvrs[c]])
                    K.op("act", lambda e: e.activation(out=cvt[:], in_=cvt[:], func=AF.Silu), reads=cvrs, writes=cvrs)
                    cut(1)
                    for c in range(8):
                        sq_, sqr_ = T.next()
                        K.op("pool", lambda e, c=c, sq_=sq_: e.tensor_tensor(out=sq_[:], in0=cvt[:, c, :], in1=cvt[:, c, :], op=ALU.mult),
                             reads=[cvrs[c]], writes=[sqr_])
                        b = bank()
                        K.op("pe", lambda e, b=b, sq_=sq_: e.matmul(psum[b][:, :], onesF, sq_[:], start=True, stop=True),
                             reads=[sqr_, cr], writes=[psr[b]])
                        rs_, rsr_ = T.next()
                        rstd_from_ps(b, lnt, rs_[:], rsr_, epsc)
                        sc = (128.0 ** -0.5) if c < 4 else 1.0
                        K.op("dve", lambda e, c=c, rs_=rs_, sc=sc: e.scalar_tensor_tensor(
                            out=cvt[:, c, :], in0=cvt[:, c, :], scalar=sc, in1=rs_[:], op0=ALU.mult, op1=ALU.mult),
                            reads=[cvrs[c], rsr_], writes=[cvrs[c]])
                    cut(2)
                    K.dma("sp", abt[:], projT[3072:3088, c0:c0 + 512], writes=[gres])
                    K.op("act", lambda e: e.activation(out=e1[:], in_=abt[:], func=AF.Exp, bias=abp[:, 1:2]), reads=[gres, cr], writes=[gres])
                    K.op("act", lambda e: e.activation(out=e1[:], in_=e1[:], func=AF.Ln, bias=one16[0:16, :]), reads=[gres, cr], writes=[gres])
                    K.op("dve", lambda e: e.tensor_scalar(out=gT[:], in0=e1[:], scalar1=nexpA[:, 0:1], scalar2=None, op0=ALU.mult),
                         reads=[gres, cr], writes=[gres])
                    K.op("act", lambda e: e.activation(out=bT[:], in_=abt[:], func=AF.Exp, scale=-1.0), reads=[gres], writes=[gres])
                    K.op("dve", lambda e: e.tensor_scalar(out=bT[:], in0=bT[:], scalar1=1.0, scalar2=None, op0=ALU.add), reads=[gres], writes=[gres])
                    K.op("dve", lambda e: e.reciprocal(out=bT[:], in_=bT[:]), reads=[gres], writes=[gres])
                    cut(3)
                    for j in range(4):
                        cg = (c0 + j * 128) // 128
                        cs = slice(j * 128, (j + 1) * 128)
                        kt_, ktr = ktok.next()
                        vt_, vtr = vtok.next()
                        for (dst, dres, ch0) in ((kt_, ktr, 4), (vt_, vtr, 8)):
                            b = bank()
                            for h in range(4):
                                K.op("pe", lambda e, b=b, h=h, ch0=ch0, cs=cs: e.transpose(
                                    psum[b][:, h * 128:(h + 1) * 128], cvt[:, ch0 + h, cs], ident[:]),
                                    reads=[cvrs[ch0 + h], cres], writes=[psr[b]])
                            copy_ps(dst[:], b, dres)
                        gk_, gkr = gtk.next()
                        b = bank()
                        K.op("pe", lambda e, b=b, cs=cs: e.transpose(psum[b][:, 0:16], gT[0:16, cs], ident[0:16, 0:16]),
                             reads=[gres, cres], writes=[psr[b]])
                        K.op("pe", lambda e, b=b, cs=cs: e.transpose(psum[b][:, 16:32], bT[0:16, cs], ident[0:16, 0:16]),
                             reads=[gres, cres], writes=[psr[b]])
                        K.op("dve", lambda e, b=b, gk_=gk_: e.tensor_copy(out=gk_[:], in_=psum[b][:, 0:32]), reads=[psr[b]], writes=[gkr])
                        kk_, kkr = kks.next()
                        qk_, qkr = qks.next()
                        b = bank()
                        for h in range(4):
                            K.op("pe", lambda e, b=b, h=h, cs=cs: e.matmul(psum[b][:, h * 128:(h + 1) * 128], cvt[:, 4 + h, cs], cvt[:, 4 + h, cs],
                                                                         start=True, stop=True), reads=[cvrs[4 + h]], writes=[psr[b]])
                        copy_ps(kk_[:], b, kkr)
                        b = bank()
                        for h in range(4):
                            K.op("pe", lambda e, b=b, h=h, cs=cs: e.matmul(psum[b][:, h * 128:(h + 1) * 128], cvt[:, 4 + h, cs], cvt[:, h, cs],
                                                                         start=True, stop=True), reads=[cvrs[4 + h], cvrs[h]], writes=[psr[b]])
                        copy_ps(qk_[:], b, qkr)
                        cut(4)
                        units = []
                        for dd in range(2):
                            last = 127 if dd == 0 else 0
                            g4 = gk_[:, dd * 4:(dd + 1) * 4]
                            beta4 = gk_[:, 24 + dd * 4:24 + (dd + 1) * 4]
                            s1, s1r = sm.next()
                            b = bank()
                            K.op("pe", lambda e, b=b, dd=dd, g4=g4: e.matmul(psum[b][:, 0:4], Mdir[dd], g4, start=True, stop=True),
                                 reads=[gkr, cr], writes=[psr[b]])
                            K.op("dve", lambda e, b=b, s1=s1: e.tensor_copy(out=s1[:, 0:4], in_=psum[b][:, 0:4]), reads=[psr[b]], writes=[s1r])
                            K.op("dve", lambda e, s1=s1: e.tensor_scalar(out=s1[:, 4:8], in0=s1[:, 0:4], scalar1=-1.0, scalar2=None, op0=ALU.mult),
                                 reads=[s1r], writes=[s1r])
                            K.op("act", lambda e, s1=s1: e.activation(out=s1[:, 8:12], in_=s1[:, 0:4], func=AF.Exp), reads=[s1r], writes=[s1r])
                            K.op("dve", lambda e, s1=s1, beta4=beta4: e.tensor_tensor(out=s1[:, 8:12], in0=s1[:, 8:12], in1=beta4, op=ALU.mult),
                                 reads=[s1r, gkr], writes=[s1r])
                            K.op("dve", lambda e, s1=s1, beta4=beta4: e.tensor_scalar(out=s1[:, 12:16], in0=beta4, scalar1=-1.0, scalar2=None, op0=ALU.mult),
                                 reads=[gkr], writes=[s1r])
                            cut(41)
                            mg_, mgr = T.next()
                            for h in range(4):
                                K.op("pool", lambda e, h=h, mg_=mg_, dd=dd, g4=g4: e.tensor_scalar(
                                    out=mg_[:, h * 128:(h + 1) * 128], in0=Mdir[dd], scalar1=g4[:, h:h + 1], scalar2=None, op0=ALU.mult),
                                    reads=[gkr, cr], writes=[mgr])
                            bU, bA, bB = bank(), bank(), bank()
                            K.op("pe", lambda e, bU=bU, mg_=mg_: e.matmul(psum[bU][:, :], onesF, mg_[:], start=True, stop=True),
                                 reads=[mgr, cr], writes=[psr[bU]])
                            K.op("pe", lambda e, bA=bA, mg_=mg_: e.matmul(psum[bA][:, :], onesF, mg_[:], start=True, stop=False),
                                 reads=[mgr, cr], writes=[psr[bA]])
                            K.op("pe", lambda e, bA=bA, dd=dd: e.matmul(psum[bA][:, :], ident[:], maskA[dd], start=False, stop=True),
                                 reads=[cr, cres], writes=[psr[bA]])
                            K.op("pe", lambda e, bB=bB, mg_=mg_: e.matmul(psum[bB][:, :], onesF, mg_[:], start=True, stop=False),
                                 reads=[mgr, cr], writes=[psr[bB]])
                            K.op("pe", lambda e, bB=bB, dd=dd: e.matmul(psum[bB][:, :], ident[:], maskB[dd], start=False, stop=True),
                                 reads=[cr, cres], writes=[psr[bB]])
                            cut(42)
                            ot_, otr = OUT.next()
                            gam_, gamr = T.next()
                            K.op("act", lambda e, bU=bU, gam_=gam_: e.activation(out=gam_[:], in_=psum[bU][:, :], func=AF.Exp),
                                 reads=[psr[bU]], writes=[gamr])
                            K.op("act", lambda e, bU=bU, s1=s1, last=last: e.activation(
                                out=s1[:, 16:20], in_=psum[bU][:, :].rearrange("p (h t) -> p h t", t=128)[:, :, last], func=AF.Exp),
                                reads=[psr[bU]], writes=[s1r])
                            cut(43)
                            for h in range(4):
                                K.op("act", lambda e, bU=bU, s1=s1, h=h, last=last: e.activation(
                                    out=s1[:, 20 + h:21 + h], in_=psum[bU][:, h * 128 + last:h * 128 + last + 1], func=AF.Exp,
                                    bias=s1[:, 4 + h:5 + h]), reads=[psr[bU], s1r], writes=[s1r])
                            cut(44)
                            dl_, dlr = T.next()
                            dt_, dtr = T.next()
                            for h in range(4):
                                hs = slice(h * 128, (h + 1) * 128)
                                K.op("act", lambda e, bA=bA, dl_=dl_, s1=s1, h=h, hs=hs: e.activation(
                                    out=dl_[:, hs], in_=psum[bA][:, hs], func=AF.Exp, bias=s1[:, h:h + 1], scale=-1.0),
                                    reads=[psr[bA], s1r], writes=[dlr])
                                K.op("act", lambda e, bB=bB, dt_=dt_, s1=s1, h=h, hs=hs: e.activation(
                                    out=dt_[:, hs], in_=psum[bB][:, hs], func=AF.Exp, bias=s1[:, 4 + h:5 + h], scale=1.0),
                                    reads=[psr[bB], s1r], writes=[dtr])
                            cut(45)
                            pt_, ptr_ = DED[dd]["pt0"]
                            for h in range(4):
                                hs = slice(h * 128, (h + 1) * 128)
                                K.op("dve", lambda e, pt_=pt_, kk_=kk_, s1=s1, dl_=dl_, h=h, hs=hs: e.scalar_tensor_tensor(
                                    out=pt_[:, hs], in0=kk_[:, hs], scalar=s1[:, 12 + h:13 + h], in1=dl_[:, hs], op0=ALU.mult, op1=ALU.mult),
                                    reads=[kkr, s1r, dlr], writes=[ptr_])
                            cut(46)
                            K.op("dve", lambda e, ot_=ot_, gam_=gam_, cs=cs: e.tensor_tensor(
                                out=ot_[:, 2, :].rearrange("p (h t) -> p h t", t=128), in0=cvt[:, 0:4, cs],
                                in1=gam_[:].rearrange("p (h t) -> p h t", t=128), op=ALU.mult), reads=cvrs[0:4] + [gamr], writes=[otr])
                            K.op("pool", lambda e, ot_=ot_, qk_=qk_, dt_=dt_: e.tensor_tensor(out=ot_[:, 3, :], in0=qk_[:], in1=dt_[:], op=ALU.mult),
                                 reads=[qkr, dtr], writes=[otr])
                            bv_, bvr = DED[dd]["bv"]
                            bgk_, bgkr = DED[dd]["bgk"]
                            for h in range(4):
                                hs = slice(h * 128, (h + 1) * 128)
                                K.op("pool", lambda e, ot_=ot_, kt_=kt_, s1=s1, h=h, hs=hs: e.tensor_scalar(
                                    out=ot_[:, 4, hs], in0=kt_[:, hs], scalar1=s1[:, 20 + h:21 + h], scalar2=None, op0=ALU.mult),
                                    reads=[ktr, s1r], writes=[otr])
                                K.op("pool", lambda e, bv_=bv_, vt_=vt_, beta4=beta4, h=h, hs=hs: e.tensor_scalar(
                                    out=bv_[:, hs], in0=vt_[:, hs], scalar1=beta4[:, h:h + 1], scalar2=None, op0=ALU.mult),
                                    reads=[vtr, gkr], writes=[bvr])
                                K.op("pool", lambda e, bgk_=bgk_, kt_=kt_, s1=s1, h=h, hs=hs: e.tensor_scalar(
                                    out=bgk_[:, hs], in0=kt_[:, hs], scalar1=s1[:, 8 + h:9 + h], scalar2=None, op0=ALU.mult),
                                    reads=[ktr, s1r], writes=[bgkr])
                            cut(47)
                            units.append(dict(dd=dd, pt=(pt_, ptr_), ot=(ot_, otr), bv=(bv_, bvr), bgk=(bgk_, bgkr), s1=(s1, s1r), cg=cg))
                        cut(5)
                        for u in units:
                            pt_, ptr_ = u["pt"]
                            b = bank()
                            for h in range(4):
                                hs = slice(h * 128, (h + 1) * 128)
                                K.op("pe", lambda e, b=b, pt_=pt_, hs=hs: e.transpose(psum[b][:, hs], pt_[:, hs], ident[:]),
                                     reads=[ptr_, cres], writes=[psr[b]])
                            p_, pr_ = DED[u["dd"]]["p0"]
                            x_, xr_ = DED[u["dd"]]["x"]
                            cut(51)
                            K.op("act", lambda e, b=b, p_=p_: e.activation(out=p_[:], in_=psum[b][:, :], func=AF.Copy), reads=[psr[b]], writes=[pr_])
                            cut(52)
                            K.op("dve", lambda e, p_=p_, x_=x_: e.tensor_tensor(out=x_[:], in0=p_[:], in1=ident4, op=ALU.add),
                                 reads=[pr_, cr], writes=[xr_])
                            u["p"] = (p_, pr_)
                            u["x"] = (x_, xr_)
                        cut(6)
                        for k in range(1, 7):
                            for u in units:
                                p_, pr_ = u["p"]
                                pt_, ptr_ = u["pt"]
                                x_, xr_ = u["x"]
                                if k < 6:
                                    b = bank()
                                    for h in range(4):
                                        hs = slice(h * 128, (h + 1) * 128)
                                        K.op("pe", lambda e, b=b, pt_=pt_, p_=p_, hs=hs: e.matmul(psum[b][:, hs], pt_[:, hs], p_[:, hs], start=True, stop=True),
                                             reads=[ptr_, pr_], writes=[psr[b]])
                                    pn_, pnr_ = DED[u["dd"]]["p%d" % (k % 2)]
                                    K.op("act", lambda e, b=b, pn_=pn_: e.activation(out=pn_[:], in_=psum[b][:, :], func=AF.Copy), reads=[psr[b]], writes=[pnr_])
                                b2 = bank()
                                for h in range(4):
                                    hs = slice(h * 128, (h + 1) * 128)
                                    K.op("pe", lambda e, b2=b2, pt_=pt_, p_=p_, hs=hs: e.matmul(psum[b2][:, hs], p_[:, hs], pt_[:, hs], start=True, stop=True),
                                         reads=[ptr_, pr_], writes=[psr[b2]])
                                ptn_, ptnr_ = DED[u["dd"]]["pt%d" % (k % 2)]
                                K.op("dve", lambda e, b2=b2, ptn_=ptn_: e.tensor_copy(out=ptn_[:], in_=psum[b2][:, :]), reads=[psr[b2]], writes=[ptnr_])
                                b3 = bank()
                                for h in range(4):
                                    hs = slice(h * 128, (h + 1) * 128)
                                    K.op("pe", lambda e, b3=b3, ptn_=ptn_, x_=x_, hs=hs: e.matmul(psum[b3][:, hs], ptn_[:, hs], x_[:, hs], start=True, stop=True),
                                         reads=[ptnr_, xr_], writes=[psr[b3]])
                                K.op("dve", lambda e, b3=b3, x_=x_: e.tensor_tensor(out=x_[:], in0=x_[:], in1=psum[b3][:, :], op=ALU.add),
                                     reads=[psr[b3], xr_], writes=[xr_])
                                if k < 6:
                                    u["p"] = (pn_, pnr_)
                                u["pt"] = (ptn_, ptnr_)
                        cut(7)
                        for u in units:
                            x_, xr_ = u["x"]
                            ot_, otr = u["ot"]
                            bv_, bvr = u["bv"]
                            bgk_, bgkr = u["bgk"]
                            s1, s1r = u["s1"]
                            b = bank()
                            for h in range(4):
                                hs = slice(h * 128, (h + 1) * 128)
                                K.op("pe", lambda e, b=b, bgk_=bgk_, x_=x_, hs=hs: e.matmul(psum[b][:, hs], bgk_[:, hs], x_[:, hs], start=True, stop=True),
                                     reads=[bgkr, xr_], writes=[psr[b]])
                            K.op("act", lambda e, b=b, ot_=ot_: e.activation(out=ot_[:, 0, :], in_=psum[b][:, :], func=AF.Copy), reads=[psr[b]], writes=[otr])
                            b = bank()
                            for h in range(4):
                                hs = slice(h * 128, (h + 1) * 128)
                                K.op("pe", lambda e, b=b, bv_=bv_, x_=x_, hs=hs: e.matmul(psum[b][:, hs], x_[:, hs], bv_[:, hs], start=True, stop=True),
                                     reads=[bvr, xr_], writes=[psr[b]])
                            K.op("dve", lambda e, b=b, ot_=ot_: e.tensor_copy(out=ot_[:, 1, :], in_=psum[b][:, :]), reads=[psr[b]], writes=[otr])
                            K.dma("sp", dn_scr[u["cg"], u["dd"]].rearrange("s p f -> p s f"), ot_[:], reads=[otr])
                            K.dma("sp", dn_ge[u["cg"], u["dd"]], s1[:, 16:20], reads=[s1r])
                        cut(8)
                K.barrier()
            except StopEmit:
                K.barrier()
                cutflag[0] = True
        if cutflag[0]:
            return "cut"
        import os as _os
        if int(_os.environ.get("DN_STOP", "9")) < 2:
            return
        with ExitStack() as ph:
            S = [sb("d_S%d" % dd, [128, 4, 128], F32, ph) for dd in range(2)]
            Sr = [Res() for _ in range(2)]
            IN = [Ring(ph, "d_IN%d" % dd, 2, [128, 5, 512], F32) for dd in range(2)]
            GE = [Ring(ph, "d_GE%d" % dd, 2, [128, 4], F32) for dd in range(2)]
            U = Ring(ph, "d_U", 4, [128, 512], F32)
            OO = Ring(ph, "d_OO", 4, [128, 512], F32)
            for (t0, slen, is_s, pi) in seqs:
                nch = slen // 128
                for dd in range(2):
                    if is_s:
                        src = (sf_in if dd == 0 else sb_in)[l].rearrange("h k v -> k h v")
                        K.dma("sp", S[dd][:], src, writes=[Sr[dd]])
                    else:
                        K.op("pool", lambda e, dd=dd: e.memset(S[dd][:], 0.0), writes=[Sr[dd]])
                for step in range(nch):
                    for dd in range(2):
                        ch = step if dd == 0 else nch - 1 - step
                        cg = t0 // 128 + ch
                        in_, inr = IN[dd].next()
                        ge_, ger = GE[dd].next()
                        K.dma("sp", in_[:], dn_scr[cg, dd].rearrange("s p f -> p s f"), writes=[inr])
                        K.dma("sp", ge_[:], dn_ge[cg, dd], writes=[ger])
                        bA = bank()
                        for h in range(4):
                            hs = slice(h * 128, (h + 1) * 128)
                            K.op("pe", lambda e, bA=bA, in_=in_, dd=dd, h=h, hs=hs: e.matmul(psum[bA][:, hs], in_[:, 0, hs], S[dd][:, h, :], start=True, stop=True),
                                 reads=[inr, Sr[dd]], writes=[psr[bA]])
                        u_, ur_ = U.next()
                        K.op("dve", lambda e, bA=bA, in_=in_, u_=u_: e.tensor_tensor(out=u_[:], in0=in_[:, 1, :], in1=psum[bA][:, :], op=ALU.subtract),
                             reads=[inr, psr[bA]], writes=[ur_])
                        bB = bank()
                        for h in range(4):
                            hs = slice(h * 128, (h + 1) * 128)
                            K.op("pe", lambda e, bB=bB, in_=in_, dd=dd, h=h, hs=hs: e.matmul(psum[bB][:, hs], in_[:, 2, hs], S[dd][:, h, :], start=True, stop=False),
                                 reads=[inr, Sr[dd]], writes=[psr[bB]])
                            K.op("pe", lambda e, bB=bB, in_=in_, u_=u_, hs=hs: e.matmul(psum[bB][:, hs], in_[:, 3, hs], u_[:, hs], start=False, stop=True),
                                 reads=[inr, ur_], writes=[psr[bB]])
                        bC = bank()
                        for h in range(4):
                            hs = slice(h * 128, (h + 1) * 128)
                            K.op("pe", lambda e, bC=bC, in_=in_, u_=u_, hs=hs: e.matmul(psum[bC][:, hs], in_[:, 4, hs], u_[:, hs], start=True, stop=True),
                                 reads=[inr, ur_], writes=[psr[bC]])
                        o_, or_ = OO.next()
                        K.op("act", lambda e, bB=bB, o_=o_: e.activation(out=o_[:], in_=psum[bB][:, :], func=AF.Copy), reads=[psr[bB]], writes=[or_])
                        K.dma("sp", Osc[dd, cg * 128:(cg + 1) * 128, :], o_[:], reads=[or_])
                        for h in range(4):
                            hs = slice(h * 128, (h + 1) * 128)
                            K.op("dve", lambda e, bC=bC, dd=dd, h=h, hs=hs, ge_=ge_: e.scalar_tensor_tensor(
                                out=S[dd][:, h, :], in0=S[dd][:, h, :], scalar=ge_[:, h:h + 1], in1=psum[bC][:, hs], op0=ALU.mult, op1=ALU.add),
                                reads=[Sr[dd], ger, psr[bC]], writes=[Sr[dd]])
                if not is_s:
                    for dd in range(2):
                        dst = (new_sf if dd == 0 else new_sb)[pi, l].rearrange("h k v -> k h v")
                        K.dma("sp", dst, S[dd][:], reads=[Sr[dd]])
            K.barrier()
        if int(_os.environ.get("DN_STOP", "9")) < 3:
            return
        with ExitStack() as ph:
            pvt = sb("f_pv", [128, 8], F32, ph)
            epsc = sb("f_eps", [128, 1], F32, ph)
            cr = Res()
            K.dma("sp", pvt[:], pv[l], writes=[cr])
            K.op("pool", lambda e: e.memset(epsc[:], EPS), writes=[cr])
            gt = Ring(ph, "f_gt", 2, [128, 4, 512], F32)
            of = Ring(ph, "f_of", 3, [128, 512], F32)
            ob = Ring(ph, "f_ob", 3, [128, 512], F32)
            sq = Ring(ph, "f_sq", 2, [128, 512], F32)
            sm = Ring(ph, "f_sm", 3, [128, 8], F32)
            yb = Ring(ph, "f_yb", 2, [128, 4, 512], BF16)
            ybTv = ybT.rearrange("(k p) t -> p k t", p=128)
            for (c0, ncols, is_s) in cfg.sub:
                g_, gr_ = gt.next()
                K.dma("sp", g_[:], pTc(20, 24, c0, c0 + 512), writes=[gr_])
                K.op("act", lambda e, g_=g_: e.activation(out=g_[:], in_=g_[:], func=AF.Silu), reads=[gr_], writes=[gr_])
                y_, yr_ = yb.next()
                for j in range(4):
                    tk = c0 + j * 128
                    f_, fr_ = of.next()
                    b_, br_ = ob.next()
                    K.dma("sp", f_[:], Osc[0, tk:tk + 128, :], writes=[fr_])
                    K.dma("sp", b_[:], Osc[1, tk:tk + 128, :], writes=[br_])
                    K.op("pool", lambda e, f_=f_, b_=b_: e.tensor_tensor(out=f_[:], in0=f_[:], in1=b_[:], op=ALU.add), reads=[fr_, br_], writes=[fr_])
                    q_, qr_ = sq.next()
                    K.op("pool", lambda e, f_=f_, q_=q_: e.tensor_tensor(out=q_[:], in0=f_[:], in1=f_[:], op=ALU.mult), reads=[fr_], writes=[qr_])
                    s_, sr_ = sm.next()
                    K.op("dve", lambda e, s_=s_, q_=q_: e.reduce_sum(out=s_[:, 0:4], in_=q_[:].rearrange("p (h e) -> p h e", e=128),
                                                                   axis=mybir.AxisListType.X), reads=[qr_], writes=[sr_])
                    K.op("act", lambda e, s_=s_: e.activation(out=s_[:, 4:8], in_=s_[:, 0:4], func=AF.Ln, bias=epsc[:], scale=1.0 / 128),
                         reads=[sr_, cr], writes=[sr_])
                    K.op("act", lambda e, s_=s_: e.activation(out=s_[:, 0:4], in_=s_[:, 4:8], func=AF.Exp, scale=-0.5), reads=[sr_], writes=[sr_])
                    for h in range(4):
                        hs = slice(h * 128, (h + 1) * 128)
                        K.op("dve", lambda e, f_=f_, s_=s_, h=h, hs=hs: e.tensor_scalar(out=f_[:, hs], in0=f_[:, hs], scalar1=s_[:, h:h + 1], scalar2=None,
                                                                                      op0=ALU.mult), reads=[fr_, sr_], writes=[fr_])
                    bt = bank()
                    for h in range(4):
                        hs = slice(h * 128, (h + 1) * 128)
                        K.op("pe", lambda e, bt=bt, f_=f_, hs=hs: e.transpose(psum[bt][:, hs], f_[:, hs], ident[:]), reads=[fr_, cres], writes=[psr[bt]])
                    K.op("dve", lambda e, bt=bt, y_=y_, g_=g_, j=j: e.scalar_tensor_tensor(
                        out=y_[:, :, j * 128:(j + 1) * 128], in0=psum[bt][:, :].rearrange("p (h t) -> p h t", t=128), scalar=pvt[:, 4:5],
                        in1=g_[:, :, j * 128:(j + 1) * 128], op0=ALU.mult, op1=ALU.mult), reads=[psr[bt], gr_, cr], writes=[yr_])
                K.dma("sp", ybTv[:, 0:4, c0:c0 + 512], y_[:], reads=[yr_])
            K.barrier()

    import os as _os2
    if _os2.environ.get("ONLY_DNET"):
        if phase_dnet(0) != "cut":
            phase_transpose_out()
        es.close()
        return nc
    phase_transpose_in()
    for l in range(L):
        phase_mod(l)
        phase_norm(0, 1)
        if cfg.nphase >= 2:
            phase_proj(l)
        if cfg.nphase >= 3:
            phase_gmlp(l)
            phase_attn(l)
        if cfg.nphase >= 4:
            phase_dnet(l)
        if cfg.nphase >= 5:
            phase_merge(l)
            phase_resid(l, mrgT, KC, w_o[l], 2, "wo")
        if cfg.nphase >= 6:
            phase_norm(3, 4)
            phase_gateup(l)
            phase_resid(l, hidT, FC, w_down[l], 5, "dn")
    phase_transpose_out()
    es.close()
    return nc


BIGM = 30000.0


def host_consts(LS):
    c = {}
    c["ident"] = np.eye(128, dtype=np.float32)
    n = max(LS, 64)
    rows = n // 64
    row = np.repeat(np.arange(rows, dtype=np.float32), 64)
    col = np.tile(np.arange(64, dtype=np.float32), rows)
    inv = (1.0 / (np.float32(10000.0) ** (np.arange(0, 64, 2, dtype=np.float32) / np.float32(64)))).astype(np.float32)
    ang = np.stack([row[:, None] * inv, col[:, None] * inv], axis=1).astype(np.float32)
    cos, sin = np.cos(ang), np.sin(ang)
    cT = np.zeros((128, n), np.float32)
    sT = np.zeros((128, n), np.float32)
    perm = np.zeros((128, 128), np.float32)
    for a in range(2):
        for b in range(2):
            for i in range(32):
                p = a * 64 + b * 32 + i
                cT[p] = cos[:, a, i]
                sT[p] = sin[:, a, i] * (-1.0 if b == 0 else 1.0)
                perm[a * 64 + (1 - b) * 32 + i, p] = 1.0
    c["ropeC"] = np.ascontiguousarray(cT[:, :LS])
    c["ropeS"] = np.ascontiguousarray(sT[:, :LS])
    c["perm"] = perm
    j = np.arange(128)[:, None]
    t = np.arange(128)[None, :]
    dm = np.zeros((7, 128, 512), np.float32)
    dm[0, :, 0:128] = (j <= t)
    dm[0, :, 128:256] = (j >= t)
    dm[1] = np.tile(BIGM * (t >= j), (1, 4))
    dm[2] = np.tile(BIGM * (t <= j), (1, 4))
    dm[3] = np.tile(-BIGM * (j > t), (1, 4))
    dm[4] = np.tile(-BIGM * (j < t), (1, 4))
    dm[5] = np.tile(np.eye(128, dtype=np.float32), (1, 4))
    dm[6] = 1.0
    c["dmask"] = dm.astype(np.float32)
    return c


def host_params(inp):
    L = inp["w_in"].shape[0]
    f = lambda a: np.ascontiguousarray(np.asarray(a, dtype=np.float32))
    o = {}
    o["b_adaT"] = f(np.asarray(inp["b_ada"]).reshape(L, 192, 128).transpose(0, 2, 1))
    o["a_wsT"] = f(np.asarray(inp["a_w_s"]).transpose(0, 3, 1, 2))
    o["bs_row"] = f(np.asarray(inp["a_b_s"]).reshape(L, 1, 512))
    pvv = np.zeros((L, 128, 8), np.float32)
    pvv[:, :, 0:4] = np.asarray(inp["a_v_gain"]).reshape(L, 4, 128).transpose(0, 2, 1)
    pvv[:, :, 4] = np.asarray(inp["b_o_gain"])
    pvv[:, :, 5] = np.asarray(inp["c_q_gain"])
    pvv[:, :, 6] = np.asarray(inp["c_k_gain"])
    o["pv"] = pvv
    o["b_convT"] = f(np.asarray(inp["b_conv"]).reshape(L, 5, 12, 128).transpose(0, 3, 2, 1))
    abp = np.zeros((L, 16, 2), np.float32)
    abp[:, 0:8, 0] = np.asarray(inp["b_a_log"]).reshape(L, 8)
    abp[:, 0:8, 1] = np.asarray(inp["b_dt_bias"]).reshape(L, 8)
    o["ab_par"] = abp
    for k in ("w_ada", "w_in", "w_mg", "w_br_a", "w_br_b", "w_br_c", "w_o", "w_gate", "w_up", "w_down"):
        o[k] = f(inp[k])
    return o


def core_inputs(inp, shared, consts, core, n_prompt_per_core):
    f = lambda a: np.ascontiguousarray(np.asarray(a, dtype=np.float32))
    L = inp["w_in"].shape[0]
    m = dict(shared)
    m.update(consts)
    m["xs"] = f(inp["x_sample"][core])
    p0 = core * n_prompt_per_core
    m["xp"] = f(np.asarray(inp["x_prompt"][p0:p0 + n_prompt_per_core]).reshape(-1, D))
    c_ctx = np.asarray(inp["c_ctx"]).reshape(32, 128).T
    cc = np.asarray(inp["c"][core]).reshape(32, 128).T
    m["condT"] = f(np.stack([c_ctx, cc], axis=-1))
    m["ck"] = f(np.asarray(inp["cache_k"][core]).reshape(L, 256, 256))
    m["cv"] = f(np.asarray(inp["cache_v"][core]).reshape(L, 256, 256))
    m["sf_in"] = f(inp["state_fwd"][core])
    m["sb_in"] = f(inp["state_bwd"][core])
    return m


_NC_CACHE = {}


def kernel(**inputs):
    n = 8
    LS = inputs["x_sample"].shape[1]
    depth = inputs["w_in"].shape[0]
    npc = inputs["x_prompt"].shape[0] // n
    cfg = Cfg(depth=depth, LS=LS, NP=npc)
    key = (depth, LS, npc)
    if key not in _NC_CACHE:
        _NC_CACHE[key] = build(cfg)
    nc = _NC_CACHE[key]
    shared = host_params(inputs)
    consts = host_consts(LS)
    in_maps = [core_inputs(inputs, shared, consts, c, npc) for c in range(n)]
    res = run_bass_kernel_spmd(nc, in_maps, core_ids=list(range(n)))
    r = res.results
    B = inputs["x_prompt"].shape[0]
    y_sample = np.stack([r[c]["ys"] for c in range(n)], 0).astype(np.float32)
    y_prompt = np.concatenate([r[c]["yp"].reshape(npc, LP, D) for c in range(n)], 0).astype(np.float32)
    nk = np.concatenate([r[c]["new_k"].reshape(npc, depth, LP, 2, 128) for c in range(n)], 0).astype(np.float32)
    nv = np.concatenate([r[c]["new_v"].reshape(npc, depth, LP, 2, 128) for c in range(n)], 0).astype(np.float32)
    nsf = np.concatenate([r[c]["new_sf"] for c in range(n)], 0).astype(np.float32)
    nsb = np.concatenate([r[c]["new_sb"] for c in range(n)], 0).astype(np.float32)
    return (y_prompt, y_sample, nk, nv, nsf, nsb)
```

```python
import numpy as np
from contextlib import ExitStack
import concourse.bass as bass
import concourse.mybir as mybir
from concourse.bass_utils import run_bass_kernel_spmd

F32 = mybir.dt.float32
BF16 = mybir.dt.bfloat16
AF = mybir.ActivationFunctionType
ALU = mybir.AluOpType

D = 4096
KC = 32
N_IN = 4880
FFN = 5632
FC = 44
LP = 256
EPS = 1e-6
DM = 8


class Res:
    __slots__ = ("w", "r")

    def __init__(self):
        self.w = {}
        self.r = {}


class KB:
    def __init__(self, nc, es):
        self.nc = nc
        self.E = {"pe": nc.tensor, "act": nc.scalar, "dve": nc.vector, "pool": nc.gpsimd, "sp": nc.sync}
        self.csem = {e: es.enter_context(nc.semaphore("c_" + e)) for e in ("pe", "act", "dve", "pool")}
        self.ccnt = {e: 0 for e in self.csem}
        self.dsem = {q: [es.enter_context(nc.semaphore("d_%s%d" % (q, i))) for i in range(DM)] for q in ("sp", "pool")}
        self.dcnt = {q: 0 for q in self.dsem}
        self.waited = {e: {} for e in self.E}
        self.nbank = 0

    def _wait(self, e, sem, val):
        k = id(sem)
        if self.waited[e].get(k, 0) >= val:
            return
        self.waited[e][k] = val
        self.E[e].wait_ge(sem, val)

    def _deps(self, e, reads, writes, own_sem, disjoint=False):
        evs = {}

        def add(d):
            for k, (sem, val) in d.items():
                if e == "pe" and sem is own_sem:
                    continue
                if k not in evs or evs[k][1] < val:
                    evs[k] = (sem, val)

        for r in reads:
            add(r.w)
        for w in writes:
            add(w.r)
            if not disjoint:
                add(w.w)
        for sem, val in evs.values():
            self._wait(e, sem, val)

    @staticmethod
    def _rec(reads, writes, sem, val):
        k = id(sem)
        for r in reads:
            r.r[k] = (sem, val)
        for w in writes:
            w.w[k] = (sem, val)

    def op(self, e, fn, reads=(), writes=()):
        sem = self.csem[e]
        self._deps(e, reads, writes, sem)
        ins = fn(self.E[e])
        self.ccnt[e] += 1
        ins.then_inc(sem, 1)
        self._rec(reads, writes, sem, self.ccnt[e])

    def dma(self, q, out, in_, reads=(), writes=(), disjoint=False):
        i = self.dcnt[q]
        slot, gen = i % DM, i // DM
        sem = self.dsem[q][slot]
        self._deps(q, reads, writes, None, disjoint)
        if gen > 0:
            self._wait(q, sem, 16 * gen)
        self.E[q].dma_start(out=out, in_=in_).then_inc(sem, 16)
        self.dcnt[q] += 1
        self._rec(reads, writes, sem, 16 * (gen + 1))

    def barrier(self):
        evs = []
        for e, sem in self.csem.items():
            if self.ccnt[e]:
                evs.append((sem, self.ccnt[e]))
        for q, sems in self.dsem.items():
            n = self.dcnt[q]
            for s in range(DM):
                cnt = (n - s + DM - 1) // DM if n > s else 0
                if cnt:
                    evs.append((sems[s], 16 * cnt))
        for e in self.E:
            for sem, val in evs:
                self._wait(e, sem, val)


class Cfg:
    def __init__(self, depth=4, LS=4096, NP=2, debug=False, nphase=99):
        self.depth, self.LS, self.NP, self.debug, self.nphase = depth, LS, NP, debug, nphase
        self.LPT = NP * LP
        self.LT = LS + self.LPT
        self.tiles = []
        c = 0
        while c < LS:
            n = min(1024, LS - c)
            self.tiles.append((c, n, True))
            c += n
        if self.LPT:
            self.tiles.append((LS, self.LPT, False))
        self.sub = []
        for (c0, n, s) in self.tiles:
            for j in range(n // 512):
                self.sub.append((c0 + j * 512, 512, s))


def build(cfg):
    nc = bass.Bass("TRN2", target_bir_lowering=False)
    es = ExitStack()
    L, LS, LT, NPR = cfg.depth, cfg.LS, cfg.LT, cfg.NP

    def din(name, shape, dt=F32):
        return nc.dram_tensor(name, list(shape), dt, kind="ExternalInput").ap()

    def dout(name, shape, dt=F32):
        return nc.dram_tensor(name, list(shape), dt, kind="ExternalOutput").ap()

    def dscr(name, shape, dt=F32):
        kind = "ExternalOutput" if cfg.debug else "Internal"
        return nc.dram_tensor(name, list(shape), dt, kind=kind).ap()

    xs = din("xs", [LS, D])
    xp = din("xp", [cfg.LPT, D])
    condT = din("condT", [128, KC, 2])
    w_ada = din("w_ada", [L, D, 6 * D])
    b_adaT = din("b_adaT", [L, 128, 192])
    w_in = din("w_in", [L, D, N_IN])
    w_mg = din("w_mg", [L, 256, 3 * D])
    w_br_a = din("w_br_a", [L, 512, D])
    w_br_b = din("w_br_b", [L, 512, D])
    w_br_c = din("w_br_c", [L, 1024, D])
    w_o = din("w_o", [L, D, D])
    w_gate = din("w_gate", [L, D, FFN])
    w_up = din("w_up", [L, D, FFN])
    w_down = din("w_down", [L, FFN, D])
    ident_d = din("ident", [128, 128])
    a_wsT = din("a_wsT", [L, 128, 4, 128])
    bs_row = din("bs_row", [L, 1, 512])
    pv = din("pv", [L, 128, 8])
    ropeC = din("ropeC", [128, LS])
    ropeS = din("ropeS", [128, LS])
    perm_d = din("perm", [128, 128])
    ck = din("ck", [L, 256, 256])
    cv = din("cv", [L, 256, 256])
    b_convT = din("b_convT", [L, 128, 12, 5])
    ab_par = din("ab_par", [L, 16, 2])
    sf_in = din("sf_in", [L, 4, 128, 128])
    sb_in = din("sb_in", [L, 4, 128, 128])
    dmask = din("dmask", [7, 128, 512])
    new_sf = dout("new_sf", [cfg.NP, L, 4, 128, 128])
    new_sb = dout("new_sb", [cfg.NP, L, 4, 128, 128])
    new_k = dout("new_k", [cfg.NP, L, 256, 256])
    new_v = dout("new_v", [cfg.NP, L, 256, 256])

    ys = dout("ys", [LS, D])
    yp = dout("yp", [cfg.LPT, D])

    xT = dscr("xT", [D, LT])
    hT = dscr("hT", [D, LT], BF16)
    projT = dscr("projT", [N_IN, LT])
    yaT = dscr("yaT", [512, LT], BF16)
    ybT = dscr("ybT", [512, LT], BF16)
    ycT = dscr("ycT", [1024, LT], BF16)
    mrgT = dscr("mrgT", [D, LT], BF16)
    hidT = dscr("hidT", [FFN, LT], BF16)
    qnT = dscr("qnT", [1024, LT], BF16)
    NCH = LT // 128
    dn_scr = dscr("dn_scr", [NCH, 2, 5, 128, 512])
    dn_ge = dscr("dn_ge", [NCH, 2, 128, 4])
    Osc = dscr("Osc", [2, LT, 512])

    K = KB(nc, es)
    uid = [0]

    def sb(name, shape, dt=F32, st=es):
        uid[0] += 1
        return st.enter_context(nc.sbuf_tensor("%s_%d" % (name, uid[0]), list(shape), dt))

    psum = [es.enter_context(nc.psum_tensor("ps%d" % i, [128, 512], F32)) for i in range(8)]
    psr = [Res() for _ in range(8)]
    NWB = 4
    wbuf = [None] * NWB
    wres = [None] * NWB
    wcnt = [0]

    def walloc(ph):
        for i in range(NWB):
            wbuf[i] = sb("wbuf%d" % i, [128, 32 * 256], BF16, ph)
            wres[i] = Res()
    ident = sb("ident_sb", [128, 128])
    onesD = sb("onesD", [128, 128], BF16)
    modT = sb("modT", [128, 192, 2])
    scond = sb("scond", [128, KC, 2], BF16)
    cres = Res()
    modres = Res()

    def bank():
        b = K.nbank % 8
        K.nbank += 1
        return b

    def wnext():
        i = wcnt[0] % NWB
        wcnt[0] += 1
        return i

    def wview(i, nk, ncols):
        return wbuf[i][:, 0:nk * ncols].rearrange("p (k c) -> p k c", c=ncols)

    def wload(i, wmat, k0, nk, c0, ncols, slot0=0, view_cols=None):
        vc = view_cols or ncols
        src = wmat[k0 * 128:(k0 + nk) * 128, c0:c0 + ncols].rearrange("(k p) n -> p k n", p=128)
        dst = wbuf[i][:, slot0 * vc:(slot0 + nk) * vc].rearrange("p (k c) -> p k c", c=vc)[:, :, 0:ncols]
        K.dma("pool", dst, src, writes=[wres[i]], disjoint=(slot0 != 0))

    with ExitStack() as ph:
        tmp = sb("c_tmp", [128, KC, 2], F32, ph)
        tmp2 = sb("c_tmp2", [128, KC, 2], F32, ph)
        tr = Res()
        K.dma("sp", ident[:], ident_d[:, :], writes=[cres])
        K.dma("sp", tmp[:], condT[:, :, :], writes=[tr])
        K.op("pool", lambda e: e.memset(onesD[:], 1.0 / D), writes=[cres])
        K.op("act", lambda e: e.activation(out=tmp2[:], in_=tmp[:], func=AF.Sigmoid), reads=[tr], writes=[tr])
        K.op("dve", lambda e: e.tensor_tensor(out=scond[:], in0=tmp[:], in1=tmp2[:], op=ALU.mult), reads=[tr], writes=[cres])
        K.barrier()

    def phase_transpose_in():
        with ExitStack() as ph:
            xin = [sb("xin%d" % i, [128, D], F32, ph) for i in range(2)]
            xres = [Res() for _ in range(2)]
            stg = sb("xstg", [128, KC, 512], F32, ph)
            sres = [Res() for _ in range(4)]
            nblk = LT // 128
            for sidx, (c0, ncols, is_s) in enumerate(cfg.sub):
                for j in range(4):
                    blk = sidx * 4 + j
                    tok0 = c0 + j * 128
                    src = xs[tok0:tok0 + 128, :] if is_s else xp[tok0 - LS:tok0 - LS + 128, :]
                    xb = blk % 2
                    K.dma("sp", xin[xb][:], src, writes=[xres[xb]])
                    for g in range(8):
                        b = bank()
                        for i in range(4):
                            kc = g * 4 + i
                            K.op("pe", lambda e, b=b, i=i, kc=kc, xb=xb: e.transpose(
                                psum[b][:, i * 128:(i + 1) * 128], xin[xb][:, kc * 128:(kc + 1) * 128], ident[:]),
                                reads=[xres[xb], cres], writes=[psr[b]])
                        eng = "act" if g % 2 == 0 else "dve"
                        dst = stg[:, g * 4:(g + 1) * 4, j * 128:(j + 1) * 128]
                        srcp = psum[b][:, :].rearrange("p (k t) -> p k t", t=128)
                        if eng == "act":
                            K.op("act", lambda e, dst=dst, srcp=srcp: e.activation(out=dst, in_=srcp, func=AF.Copy),
                                 reads=[psr[b]], writes=[sres[j]])
                        else:
                            K.op("dve", lambda e, dst=dst, srcp=srcp: e.tensor_copy(out=dst, in_=srcp),
                                 reads=[psr[b]], writes=[sres[j]])
                K.dma("sp", xT.rearrange("(k p) t -> p k t", p=128)[:, :, c0:c0 + 512], stg[:], reads=sres)
            K.barrier()

    def phase_mod(l):
        with ExitStack() as ph:
            walloc(ph)
            bt = sb("badaT", [128, 192], F32, ph)
            br = Res()
            K.dma("sp", bt[:], b_adaT[l], writes=[br])
            b = bank()
            ps3 = psum[b][:, 0:384].rearrange("p (i r) -> p i r", r=2)
            for blk in range(96):
                wi = wnext()
                wload(wi, w_ada[l], 0, KC, blk * 256, 256)
                wv = wview(wi, KC, 256)
                for m in range(2):
                    idx = blk * 2 + m
                    for kc in range(KC):
                        K.op("pe", lambda e, wv=wv, m=m, kc=kc, idx=idx: e.matmul(
                            ps3[:, idx, :], wv[:, kc, m * 128:(m + 1) * 128], scond[:, kc, :],
                            start=(kc == 0), stop=(kc == KC - 1)),
                            reads=[wres[wi], cres], writes=[psr[b]])
            for r in range(2):
                K.op("dve", lambda e, r=r: e.tensor_tensor(out=modT[:, :, r], in0=ps3[:, :, r], in1=bt[:], op=ALU.add),
                     reads=[psr[b], br], writes=[modres])
            for j in (1, 4):
                K.op("dve", lambda e, j=j: e.tensor_scalar(
                    out=modT[:, j * 32:(j + 1) * 32, :], in0=modT[:, j * 32:(j + 1) * 32, :],
                    scalar1=1.0, scalar2=None, op0=ALU.add), reads=[modres], writes=[modres])
            K.barrier()

    def modcol(j, kc, is_s):
        r = 1 if is_s else 0
        return modT[:, j * 32 + kc, r:r + 1]

    def phase_norm(jshift, jscale):
        with ExitStack() as ph:
            xt = sb("n_x", [128, KC, 512], F32, ph)
            xr = [Res() for _ in range(4)]
            sq = [sb("n_sq%d" % i, [128, 512], BF16, ph) for i in range(3)]
            sqr = [Res() for _ in range(3)]
            lnt = sb("n_ln", [128, 512], F32, ph)
            rstd = sb("n_rstd", [128, 512], F32, ph)
            rr = Res()
            tmpb = [sb("n_tmp%d" % i, [128, 512], F32, ph) for i in range(3)]
            tmr = [Res() for _ in range(3)]
            ho = [sb("n_ho%d" % i, [128, 8, 512], BF16, ph) for i in range(2)]
            hr = [Res() for _ in range(2)]
            epsc = sb("n_eps", [128, 1], F32, ph)
            K.op("pool", lambda e: e.memset(epsc[:], EPS), writes=[rr])
            xTv = xT.rearrange("(k p) t -> p k t", p=128)
            hTv = hT.rearrange("(k p) t -> p k t", p=128)
            n = 0
            hn = 0
            for (c0, ncols, is_s) in cfg.sub:
                for g in range(4):
                    K.dma("sp", xt[:, g * 8:(g + 1) * 8, :], xTv[:, g * 8:(g + 1) * 8, c0:c0 + 512], writes=[xr[g]])
                b = bank()
                for kc in range(KC):
                    s = n % 3
                    n += 1
                    K.op("act", lambda e, s=s, kc=kc: e.activation(out=sq[s][:], in_=xt[:, kc, :], func=AF.Square),
                         reads=[xr[kc // 8]], writes=[sqr[s]])
                    K.op("pe", lambda e, s=s, kc=kc, b=b: e.matmul(psum[b][:, :], onesD[:], sq[s][:],
                                                                   start=(kc == 0), stop=(kc == KC - 1)),
                         reads=[sqr[s], cres], writes=[psr[b]])
                K.op("act", lambda e, b=b: e.activation(out=lnt[:], in_=psum[b][:, :], func=AF.Ln, bias=epsc[:]),
                     reads=[psr[b], rr], writes=[rr])
                K.op("act", lambda e: e.activation(out=rstd[:], in_=lnt[:], func=AF.Exp, scale=-0.5),
                     reads=[rr], writes=[rr])
                for g in range(4):
                    hb = hn % 2
                    hn += 1
                    for i in range(8):
                        kc = g * 8 + i
                        s = n % 3
                        n += 1
                        K.op("dve", lambda e, s=s, kc=kc, is_s=is_s: e.scalar_tensor_tensor(
                            out=tmpb[s][:], in0=xt[:, kc, :], scalar=modcol(jscale, kc, is_s), in1=rstd[:],
                            op0=ALU.mult, op1=ALU.mult), reads=[xr[g], rr, modres], writes=[tmr[s]])
                        K.op("dve", lambda e, s=s, kc=kc, is_s=is_s, hb=hb, i=i: e.tensor_scalar(
                            out=ho[hb][:, i, :], in0=tmpb[s][:], scalar1=modcol(jshift, kc, is_s), scalar2=None,
                            op0=ALU.add), reads=[tmr[s], modres], writes=[hr[hb]])
                    K.dma("sp", hTv[:, g * 8:(g + 1) * 8, c0:c0 + 512], ho[hb][:], reads=[hr[hb]])
            K.barrier()

    def load_act_tile(at, ares, srcT, nk, c0, ncols):
        v = srcT.rearrange("(k p) t -> p k t", p=128)
        step = 8
        for k0 in range(0, nk, step):
            k1 = min(nk, k0 + step)
            K.dma("sp", at[:, k0:k1, 0:ncols], v[:, k0:k1, c0:c0 + ncols], writes=[ares], disjoint=(k0 != 0))

    def phase_gemm(srcT, nk, wmat, blocks, epilogue, name):
        with ExitStack() as ph:
            walloc(ph)
            at = sb(name + "_act", [128, nk, 1024], BF16, ph)
            ares = Res()
            for tile in cfg.tiles:
                (c0, ncols, is_s) = tile
                NT = ncols // 512
                load_act_tile(at, ares, srcT, nk, c0, ncols)
                for (bc0, bw) in blocks:
                    wis = []
                    for k0 in range(0, nk, 32):
                        wi = wnext()
                        wload(wi, wmat, k0, min(32, nk - k0), bc0, bw, view_cols=256)
                        wis.append(wi)
                    for m0 in range(0, bw, 128):
                        mw = min(128, bw - m0)
                        pbs = [bank() for _ in range(NT)]
                        for kc in range(nk):
                            wi = wis[kc // 32]
                            wv = wview(wi, 32, 256)
                            for nt in range(NT):
                                K.op("pe", lambda e, wv=wv, kc=kc, m0=m0, mw=mw, nt=nt, pb=pbs[nt]: e.matmul(
                                    psum[pb][0:mw, :], wv[:, kc % 32, m0:m0 + mw], at[:, kc, nt * 512:(nt + 1) * 512],
                                    start=(kc == 0), stop=(kc == nk - 1)),
                                    reads=[wres[wi], ares], writes=[psr[pbs[nt]]])
                        for nt in range(NT):
                            epilogue(ph, pbs[nt], bc0 + m0, mw, c0 + nt * 512, is_s)
            K.barrier()

    class Ring:
        def __init__(self, ph, name, n, shape, dt):
            self.t = [sb("%s%d" % (name, i), shape, dt, ph) for i in range(n)]
            self.r = [Res() for _ in range(n)]
            self.i = 0

        def next(self):
            j = self.i % len(self.t)
            self.i += 1
            return self.t[j], self.r[j]

    def phase_proj(l):
        st = {}

        def epi(ph, pb, row0, mw, col0, is_s):
            if "ring" not in st:
                st["ring"] = Ring(ph, "pj_o", 4, [128, 512], F32)
                st["n"] = 0
            t, r = st["ring"].next()
            st["n"] += 1
            if st["n"] % 2:
                K.op("act", lambda e: e.activation(out=t[0:mw, :], in_=psum[pb][0:mw, :], func=AF.Copy),
                     reads=[psr[pb]], writes=[r])
            else:
                K.op("dve", lambda e: e.tensor_copy(out=t[0:mw, :], in_=psum[pb][0:mw, :]), reads=[psr[pb]], writes=[r])
            K.dma("sp", projT[row0:row0 + mw, col0:col0 + 512], t[0:mw, :], reads=[r])

        blocks = [(c, 256) for c in range(0, 3072, 256)] + [(3072, 16)] + [(c, 256) for c in range(3088, N_IN, 256)]
        phase_gemm(hT, KC, w_in[l], blocks, epi, "pj")

    def phase_resid(l, srcT, nk, wmat, jgate, name):
        st = {}

        def epi(ph, pb, row0, mw, col0, is_s):
            if "xi" not in st:
                st["xi"] = Ring(ph, name + "_xi", 4, [128, 512], F32)
                st["xo"] = Ring(ph, name + "_xo", 4, [128, 512], F32)
            ti, ri = st["xi"].next()
            to, ro = st["xo"].next()
            kc = row0 // 128
            K.dma("sp", ti[:], xT[row0:row0 + 128, col0:col0 + 512], writes=[ri])
            K.op("dve", lambda e: e.scalar_tensor_tensor(
                out=to[:], in0=psum[pb][:, :], scalar=modcol(jgate, kc, is_s), in1=ti[:], op0=ALU.mult, op1=ALU.add),
                reads=[psr[pb], ri, modres], writes=[ro])
            K.dma("sp", xT[row0:row0 + 128, col0:col0 + 512], to[:], reads=[ro])

        blocks = [(c, 256) for c in range(0, D, 256)]
        phase_gemm(srcT, nk, wmat, blocks, epi, name)

    def phase_gateup(l):
        with ExitStack() as ph:
            walloc(ph)
            at = sb("gu_act", [128, KC, 1024], BF16, ph)
            ares = Res()
            sg = Ring(ph, "gu_sg", 3, [128, 512], F32)
            ho = Ring(ph, "gu_ho", 3, [128, 512], BF16)
            for (c0, ncols, is_s) in cfg.tiles:
                NT = ncols // 512
                load_act_tile(at, ares, hT, KC, c0, ncols)
                for bc0 in range(0, FFN, 256):
                    wg = wnext()
                    wload(wg, w_gate[l], 0, KC, bc0, 256)
                    wu = wnext()
                    wload(wu, w_up[l], 0, KC, bc0, 256)
                    for m in range(2):
                        pg = [bank() for _ in range(NT)]
                        pu = [bank() for _ in range(NT)]
                        for (wi, pbs) in ((wg, pg), (wu, pu)):
                            wv = wview(wi, 32, 256)
                            for kc in range(KC):
                                for nt in range(NT):
                                    K.op("pe", lambda e, wv=wv, kc=kc, m=m, nt=nt, pb=pbs[nt]: e.matmul(
                                        psum[pb][:, :], wv[:, kc, m * 128:(m + 1) * 128],
                                        at[:, kc, nt * 512:(nt + 1) * 512], start=(kc == 0), stop=(kc == KC - 1)),
                                        reads=[wres[wi], ares], writes=[psr[pbs[nt]]])
                        for nt in range(NT):
                            ts, rs = sg.next()
                            th, rh = ho.next()
                            K.op("act", lambda e, ts=ts, pb=pg[nt]: e.activation(out=ts[:], in_=psum[pb][:, :], func=AF.Silu),
                                 reads=[psr[pg[nt]]], writes=[rs])
                            K.op("dve", lambda e, ts=ts, th=th, pb=pu[nt]: e.tensor_tensor(
                                out=th[:], in0=ts[:], in1=psum[pb][:, :], op=ALU.mult),
                                reads=[rs, psr[pu[nt]]], writes=[rh])
                            r0 = bc0 + m * 128
                            K.dma("sp", hidT[r0:r0 + 128, c0 + nt * 512:c0 + (nt + 1) * 512], th[:], reads=[rh])
            K.barrier()

    def phase_transpose_out():
        with ExitStack() as ph:
            xt = sb("to_x", [128, KC, 512], F32, ph)
            xr = Res()
            yo = [sb("to_y%d" % i, [128, D], F32, ph) for i in range(2)]
            yr = [Res() for _ in range(2)]
            xTv = xT.rearrange("(k p) t -> p k t", p=128)
            blk = 0
            for (c0, ncols, is_s) in cfg.sub:
                for g in range(4):
                    K.dma("sp", xt[:, g * 8:(g + 1) * 8, :], xTv[:, g * 8:(g + 1) * 8, c0:c0 + 512], writes=[xr],
                          disjoint=(g != 0))
                for j in range(4):
                    yb = blk % 2
                    blk += 1
                    for g in range(8):
                        b = bank()
                        for i in range(4):
                            kc = g * 4 + i
                            K.op("pe", lambda e, b=b, i=i, kc=kc, j=j: e.transpose(
                                psum[b][:, i * 128:(i + 1) * 128], xt[:, kc, j * 128:(j + 1) * 128], ident[:]),
                                reads=[xr, cres], writes=[psr[b]])
                        dst = yo[yb][:, g * 512:(g + 1) * 512]
                        if g % 2 == 0:
                            K.op("act", lambda e, dst=dst, b=b: e.activation(out=dst, in_=psum[b][:, :], func=AF.Copy),
                                 reads=[psr[b]], writes=[yr[yb]])
                        else:
                            K.op("dve", lambda e, dst=dst, b=b: e.tensor_copy(out=dst, in_=psum[b][:, :]),
                                 reads=[psr[b]], writes=[yr[yb]])
                    tok0 = c0 + j * 128
                    dstd = ys[tok0:tok0 + 128, :] if is_s else yp[tok0 - LS:tok0 - LS + 128, :]
                    K.dma("sp", dstd, yo[yb][:], reads=[yr[yb]])
            K.barrier()

    lnres = Res()

    def rstd_from_ps(pb, lnt, out, rres, epsc, n=512):
        K.op("act", lambda e: e.activation(out=lnt[:, 0:n], in_=psum[pb][:, 0:n], func=AF.Ln, bias=epsc[:]),
             reads=[psr[pb]], writes=[lnres])
        K.op("act", lambda e: e.activation(out=out, in_=lnt[:, 0:n], func=AF.Exp, scale=-0.5),
             reads=[lnres], writes=[rres])

    def phase_gmlp(l):
        with ExitStack() as ph:
            wsT = sb("g_wsT", [128, 4, 128], F32, ph)
            bsr = sb("g_bsr", [1, 512], F32, ph)
            ones1 = sb("g_ones1", [1, 128], F32, ph)
            pvt = sb("g_pv", [128, 8], F32, ph)
            onesF = sb("g_onesF", [128, 128], F32, ph)
            epsc = sb("g_eps", [128, 1], F32, ph)
            cr = Res()
            K.dma("sp", wsT[:], a_wsT[l], writes=[cr])
            K.dma("sp", bsr[:], bs_row[l], writes=[cr], disjoint=True)
            K.dma("sp", pvt[:], pv[l], writes=[cr], disjoint=True)
            K.op("pool", lambda e: e.memset(ones1[:], 1.0), writes=[cr])
            K.op("pool", lambda e: e.memset(onesF[:], 1.0 / 512), writes=[cr])
            K.op("pool", lambda e: e.memset(epsc[:], EPS), writes=[cr])
            u = [sb("g_u%d" % i, [128, 4, 512], F32, ph) for i in range(2)]
            v = [sb("g_v%d" % i, [128, 4, 512], F32, ph) for i in range(2)]
            ur = [Res() for _ in range(2)]
            vr = [Res() for _ in range(2)]
            sqv = sb("g_sq", [128, 4, 512], F32, ph)
            sqr = Res()
            lnt = sb("g_ln", [128, 512], F32, ph)
            rstd = sb("g_rstd", [128, 512], F32, ph)
            rr = Res()
            vn = sb("g_vn", [128, 4, 512], F32, ph)
            vnr = Res()
            vtok = Ring(ph, "g_vtok", 2, [128, 4, 128], F32)
            yo = Ring(ph, "g_yo", 2, [128, 4, 512], BF16)
            yaTv = yaT.rearrange("(k p) t -> p k t", p=128)
            for si, (c0, ncols, is_s) in enumerate(cfg.sub):
                i2 = si % 2
                K.dma("sp", u[i2][:], pTc(0, 4, c0, c0 + 512), writes=[ur[i2]])
                K.dma("sp", v[i2][:], pTc(4, 8, c0, c0 + 512), writes=[vr[i2]])
                K.op("act", lambda e, i2=i2: e.activation(out=u[i2][:], in_=u[i2][:], func=AF.Gelu_apprx_tanh),
                     reads=[ur[i2]], writes=[ur[i2]])
                K.op("act", lambda e, i2=i2: e.activation(out=v[i2][:], in_=v[i2][:], func=AF.Gelu_apprx_tanh),
                     reads=[vr[i2]], writes=[vr[i2]])
                K.op("pool", lambda e, i2=i2: e.tensor_tensor(out=sqv[:], in0=v[i2][:], in1=v[i2][:], op=ALU.mult),
                     reads=[vr[i2]], writes=[sqr])
                b = bank()
                for g in range(4):
                    K.op("pe", lambda e, g=g, b=b: e.matmul(psum[b][:, :], onesF[:], sqv[:, g, :],
                                                            start=(g == 0), stop=(g == 3)),
                         reads=[sqr, cr], writes=[psr[b]])
                rstd_from_ps(b, lnt, rstd[:], rr, epsc)
                for g in range(4):
                    K.op("dve", lambda e, g=g, i2=i2: e.scalar_tensor_tensor(
                        out=vn[:, g, :], in0=v[i2][:, g, :], scalar=pvt[:, g:g + 1], in1=rstd[:],
                        op0=ALU.mult, op1=ALU.mult), reads=[vr[i2], rr, cr], writes=[vnr])
                yt, yr = yo.next()
                for g in range(4):
                    bt = bank()
                    for j in range(4):
                        K.op("pe", lambda e, g=g, j=j, bt=bt: e.transpose(
                            psum[bt][:, j * 128:(j + 1) * 128], vn[:, g, j * 128:(j + 1) * 128], ident[:]),
                            reads=[vnr, cres], writes=[psr[bt]])
                    vt, vtr = vtok.next()
                    K.op("act", lambda e, vt=vt, bt=bt: e.activation(
                        out=vt[:], in_=psum[bt][:, :].rearrange("p (j c) -> p j c", c=128), func=AF.Copy),
                        reads=[psr[bt]], writes=[vtr])
                    bm = bank()
                    for j in range(4):
                        K.op("pe", lambda e, g=g, j=j, bm=bm, vt=vt: e.matmul(
                            psum[bm][:, j * 128:(j + 1) * 128], vt[:, j, :], wsT[:, g, :], start=True, stop=False),
                            reads=[vtr, cr], writes=[psr[bm]])
                        K.op("pe", lambda e, g=g, j=j, bm=bm: e.matmul(
                            psum[bm][:, j * 128:(j + 1) * 128], ones1[0:1, :], bsr[0:1, g * 128:(g + 1) * 128],
                            start=False, stop=True), reads=[cr], writes=[psr[bm]])
                    K.op("dve", lambda e, g=g, bm=bm, yt=yt, i2=i2: e.tensor_tensor(
                        out=yt[:, g, :], in0=u[i2][:, g, :], in1=psum[bm][:, :], op=ALU.mult),
                        reads=[ur[i2], psr[bm]], writes=[yr])
                K.dma("sp", yaTv[:, 0:4, c0:c0 + 512], yt[:], reads=[yr])
            K.barrier()

    SCALE = 128.0 ** -0.5

    def phase_attn(l):
        with ExitStack() as ph:
            pvt = sb("a_pv", [128, 8], F32, ph)
            onesH = sb("a_onesH", [128, 128], F32, ph)
            onesb = sb("a_onesb", [128, 128], BF16, ph)
            permt = sb("a_perm", [128, 128], F32, ph)
            epsc = sb("a_eps", [128, 1], F32, ph)
            cr = Res()
            K.dma("sp", pvt[:], pv[l], writes=[cr])
            K.dma("sp", permt[:], perm_d[:, :], writes=[cr], disjoint=True)
            K.op("pool", lambda e: e.memset(onesH[:], 1.0 / 128), writes=[cr])
            K.op("pool", lambda e: e.memset(onesb[:], 1.0), writes=[cr])
            K.op("pool", lambda e: e.memset(epsc[:], EPS), writes=[cr])
            NBS = 2 + LS // 128
            KT = sb("a_KT", [128, 2, 256 + LS], BF16, ph)
            Vt = sb("a_Vt", [128, NBS, 256], BF16, ph)
            KTp = sb("a_KTp", [128, 2, cfg.LPT], BF16, ph)
            Vp = sb("a_Vp", [128, cfg.LPT // 128, 256], BF16, ph)
            kvres = Res()
            with ExitStack() as p1:
                ckt = sb("a_ck", [128, 2, 256], F32, p1)
                cvt = sb("a_cv", [128, 2, 256], F32, p1)
                ckr = Res()
                K.dma("sp", ckt[:], ck[l].rearrange("(b p) f -> p b f", p=128), writes=[ckr])
                K.dma("sp", cvt[:], cv[l].rearrange("(b p) f -> p b f", p=128), writes=[ckr], disjoint=True)
                K.op("dve", lambda e: e.tensor_copy(out=Vt[:, 0:2, :], in_=cvt[:]), reads=[ckr], writes=[kvres])
                for kv in range(2):
                    b = bank()
                    for blk in range(2):
                        K.op("pe", lambda e, b=b, blk=blk, kv=kv: e.transpose(
                            psum[b][:, blk * 128:(blk + 1) * 128], ckt[:, blk, kv * 128:(kv + 1) * 128], ident[:]),
                            reads=[ckr, cres], writes=[psr[b]])
                    K.op("dve", lambda e, b=b, kv=kv: e.tensor_copy(out=KT[:, kv, 0:256], in_=psum[b][:, 0:256]),
                         reads=[psr[b]], writes=[kvres])
                aq = [sb("a_aq%d" % i, [128, 12, 512], F32, p1) for i in range(2)]
                aqr = [Res() for _ in range(2)]
                cs = [sb("a_cs%d" % i, [128, 2, 512], F32, p1) for i in range(2)]
                csr = [Res() for _ in range(2)]
                sq = Ring(p1, "a_sq", 2, [128, 512], F32)
                lnt = sb("a_ln", [128, 512], F32, p1)
                rs = Ring(p1, "a_rs", 2, [128, 512], F32)
                xn = Ring(p1, "a_xn", 2, [128, 512], F32)
                r1 = Ring(p1, "a_r1", 2, [128, 512], F32)
                r2 = Ring(p1, "a_r2", 2, [128, 512], F32)
                qo = Ring(p1, "a_qo", 2, [128, 8, 512], BF16)
                kst = Ring(p1, "a_kst", 2, [128, 2, 256], F32)
                vst = Ring(p1, "a_vst", 2, [128, 2, 256], F32)
                qnTv = qnT.rearrange("(k p) t -> p k t", p=128)
                for si, (c0, ncols, is_s) in enumerate(cfg.sub):
                    i2 = si % 2
                    K.dma("sp", aq[i2][:], pTv_off(3088, 12, c0), writes=[aqr[i2]])
                    if is_s:
                        K.dma("sp", cs[i2][:, 0, :], ropeC[:, c0:c0 + 512], writes=[csr[i2]])
                        K.dma("sp", cs[i2][:, 1, :], ropeS[:, c0:c0 + 512], writes=[csr[i2]], disjoint=True)
                    qt_, qr_ = qo.next()
                    if not is_s:
                        kf = sb("a_kf%d" % si, [128, 2, 512], F32, p1)
                        kfr = Res()
                    for c in range(10):
                        st_, sr_ = sq.next()
                        K.op("pool", lambda e, c=c, i2=i2, st_=st_: e.tensor_tensor(
                            out=st_[:], in0=aq[i2][:, c, :], in1=aq[i2][:, c, :], op=ALU.mult),
                            reads=[aqr[i2]], writes=[sr_])
                        b = bank()
                        K.op("pe", lambda e, b=b, st_=st_: e.matmul(psum[b][:, :], onesH[:], st_[:], start=True, stop=True),
                             reads=[sr_, cr], writes=[psr[b]])
                        rt_, rr_ = rs.next()
                        rstd_from_ps(b, lnt, rt_[:], rr_, epsc)
                        gcol = 5 if c < 8 else 6
                        if is_s:
                            xt_, xr_ = xn.next()
                            K.op("dve", lambda e, c=c, i2=i2, xt_=xt_, rt_=rt_, gcol=gcol: e.scalar_tensor_tensor(
                                out=xt_[:], in0=aq[i2][:, c, :], scalar=pvt[:, gcol:gcol + 1], in1=rt_[:],
                                op0=ALU.mult, op1=ALU.mult), reads=[aqr[i2], rr_, cr], writes=[xr_])
                            b2 = bank()
                            K.op("pe", lambda e, b2=b2, xt_=xt_: e.matmul(psum[b2][:, :], permt[:], xt_[:], start=True, stop=True),
                                 reads=[xr_, cr], writes=[psr[b2]])
                            r1t, r1r = r1.next()
                            r2t, r2r = r2.next()
                            K.op("pool", lambda e, xt_=xt_, r1t=r1t, i2=i2: e.tensor_tensor(
                                out=r1t[:], in0=xt_[:], in1=cs[i2][:, 0, :], op=ALU.mult),
                                reads=[xr_, csr[i2]], writes=[r1r])
                            K.op("dve", lambda e, b2=b2, r2t=r2t, i2=i2: e.tensor_tensor(
                                out=r2t[:], in0=psum[b2][:, :], in1=cs[i2][:, 1, :], op=ALU.mult),
                                reads=[psr[b2], csr[i2]], writes=[r2r])
                            if c < 8:
                                K.op("pool", lambda e, r1t=r1t, r2t=r2t, qt_=qt_, c=c: e.tensor_tensor(
                                    out=qt_[:, c, :], in0=r1t[:], in1=r2t[:], op=ALU.add),
                                    reads=[r1r, r2r], writes=[qr_])
                            else:
                                K.op("pool", lambda e, r1t=r1t, r2t=r2t, c=c, c0=c0: e.tensor_tensor(
                                    out=KT[:, c - 8, 256 + c0:256 + c0 + 512], in0=r1t[:], in1=r2t[:], op=ALU.add),
                                    reads=[r1r, r2r], writes=[kvres])
                        else:
                            if c < 8:
                                K.op("dve", lambda e, c=c, i2=i2, rt_=rt_, gcol=gcol, qt_=qt_: e.scalar_tensor_tensor(
                                    out=qt_[:, c, :], in0=aq[i2][:, c, :], scalar=pvt[:, gcol:gcol + 1], in1=rt_[:],
                                    op0=ALU.mult, op1=ALU.mult), reads=[aqr[i2], rr_, cr], writes=[qr_])
                            else:
                                K.op("dve", lambda e, c=c, i2=i2, rt_=rt_, gcol=gcol, kf=kf: e.scalar_tensor_tensor(
                                    out=kf[:, c - 8, :], in0=aq[i2][:, c, :], scalar=pvt[:, gcol:gcol + 1], in1=rt_[:],
                                    op0=ALU.mult, op1=ALU.mult), reads=[aqr[i2], rr_, cr], writes=[kfr])
                                K.op("pool", lambda e, c=c, kf=kf, c0=c0: e.tensor_copy(
                                    out=KTp[:, c - 8, c0 - LS:c0 - LS + 512], in_=kf[:, c - 8, :]),
                                    reads=[kfr], writes=[kvres])
                    K.dma("sp", qnTv[:, 0:8, c0:c0 + 512], qt_[:], reads=[qr_])
                    for j in range(4):
                        bv = bank()
                        for kv in range(2):
                            K.op("pe", lambda e, bv=bv, kv=kv, j=j, i2=i2: e.transpose(
                                psum[bv][:, kv * 128:(kv + 1) * 128], aq[i2][:, 10 + kv, j * 128:(j + 1) * 128], ident[:]),
                                reads=[aqr[i2], cres], writes=[psr[bv]])
                        if is_s:
                            blk = 2 + c0 // 128 + j
                            K.op("act", lambda e, bv=bv, blk=blk: e.activation(out=Vt[:, blk, :], in_=psum[bv][:, 0:256], func=AF.Copy),
                                 reads=[psr[bv]], writes=[kvres])
                        else:
                            tokp = c0 - LS + j * 128
                            pi, t0 = tokp // LP, tokp % LP
                            vt_, vr_ = vst.next()
                            K.op("act", lambda e, bv=bv, vt_=vt_: e.activation(out=vt_[:, 0, :], in_=psum[bv][:, 0:256], func=AF.Copy),
                                 reads=[psr[bv]], writes=[vr_])
                            K.op("pool", lambda e, vt_=vt_, tokp=tokp: e.tensor_copy(out=Vp[:, tokp // 128, :], in_=vt_[:, 0, :]),
                                 reads=[vr_], writes=[kvres])
                            K.dma("sp", new_v[pi, l, t0:t0 + 128, :], vt_[:, 0, :], reads=[vr_])
                            bk = bank()
                            for kv in range(2):
                                K.op("pe", lambda e, bk=bk, kv=kv, j=j, kf=kf: e.transpose(
                                    psum[bk][:, kv * 128:(kv + 1) * 128], kf[:, kv, j * 128:(j + 1) * 128], ident[:]),
                                    reads=[kfr, cres], writes=[psr[bk]])
                            kt_, kr_ = kst.next()
                            K.op("dve", lambda e, bk=bk, kt_=kt_: e.tensor_copy(out=kt_[:, 0, :], in_=psum[bk][:, 0:256]),
                                 reads=[psr[bk]], writes=[kr_])
                            K.dma("sp", new_k[pi, l, t0:t0 + 128, :], kt_[:, 0, :], reads=[kr_])
                K.barrier()
            with ExitStack() as p2:
                qn = [sb("a_qn%d" % i, [128, 8, 512], BF16, p2) for i in range(2)]
                qnr = [Res() for _ in range(2)]
                pt = Ring(p2, "a_pt", 3, [128, 512], BF16)
                rv = Ring(p2, "a_rv", 2, [128, 512], F32)
                yo = Ring(p2, "a_yo", 2, [128, 512], BF16)
                cnt = {"a": 0, "s": 0}

                def attend(qap, nq, kaps, vaps, dst):
                    ia = cnt["a"]
                    cnt["a"] += 1
                    po, pl = ia % 2, 2 + ia % 2
                    n = len(kaps)
                    sbk = {}

                    def S(kb):
                        b = 4 + cnt["s"] % 4
                        cnt["s"] += 1
                        sbk[kb] = b
                        K.op("pe", lambda e: e.matmul(psum[b][:, 0:nq], kaps[kb], qap, start=True, stop=True),
                             reads=[kvres, qres_cur[0]], writes=[psr[b]])
                    S(0)
                    for kb in range(n):
                        if kb + 1 < n:
                            S(kb + 1)
                        b = sbk[kb]
                        pt_, pr_ = pt.next()
                        K.op("act", lambda e, b=b, pt_=pt_: e.activation(out=pt_[:, 0:nq], in_=psum[b][:, 0:nq], func=AF.Exp, scale=SCALE),
                             reads=[psr[b]], writes=[pr_])
                        K.op("pe", lambda e, kb=kb, pt_=pt_: e.matmul(psum[po][:, 0:nq], vaps[kb], pt_[:, 0:nq],
                                                                    start=(kb == 0), stop=(kb == n - 1)),
                             reads=[kvres, pr_], writes=[psr[po]])
                        K.op("pe", lambda e, kb=kb, pt_=pt_: e.matmul(psum[pl][:, 0:nq], onesb[:], pt_[:, 0:nq],
                                                                    start=(kb == 0), stop=(kb == n - 1)),
                             reads=[cr, pr_], writes=[psr[pl]])
                    rv_, rvr = rv.next()
                    yo_, yor = yo.next()
                    K.op("dve", lambda e: e.reciprocal(out=rv_[:, 0:nq], in_=psum[pl][:, 0:nq]), reads=[psr[pl]], writes=[rvr])
                    K.op("dve", lambda e: e.tensor_tensor(out=yo_[:, 0:nq], in0=psum[po][:, 0:nq], in1=rv_[:, 0:nq], op=ALU.mult),
                         reads=[psr[po], rvr], writes=[yor])
                    K.dma("sp", dst, yo_[:, 0:nq], reads=[yor])

                qres_cur = [None]
                for si, (c0, ncols, is_s) in enumerate(cfg.sub):
                    i2 = si % 2
                    K.dma("sp", qn[i2][:], qnT.rearrange("(k p) t -> p k t", p=128)[:, 0:8, c0:c0 + 512], writes=[qnr[i2]])
                    qres_cur[0] = qnr[i2]
                    for h in range(8):
                        kvh = h // 4
                        if is_s:
                            kaps = [KT[:, kvh, kb * 128:(kb + 1) * 128] for kb in range(NBS)]
                            vaps = [Vt[:, kb, kvh * 128:(kvh + 1) * 128] for kb in range(NBS)]
                            attend(qn[i2][:, h, :], 512, kaps, vaps, ycT[h * 128:(h + 1) * 128, c0:c0 + 512])
                        else:
                            for pi in range(512 // LP):
                                t0 = c0 - LS + pi * LP
                                kaps = [KTp[:, kvh, t0 + kb * 128:t0 + (kb + 1) * 128] for kb in range(2)]
                                vaps = [Vp[:, t0 // 128 + kb, kvh * 128:(kvh + 1) * 128] for kb in range(2)]
                                attend(qn[i2][:, h, pi * LP:(pi + 1) * LP], LP, kaps, vaps,
                                       ycT[h * 128:(h + 1) * 128, c0 + pi * LP:c0 + (pi + 1) * LP])
                K.barrier()

    def pTc(ch0, ch1, a, b):
        return projT[(ch0) * 128:(ch1) * 128, a:b].rearrange("(k p) t -> p k t", p=128)

    def pTv_off(row0, nch, c0):
        return projT[row0:row0 + nch * 128, c0:c0 + 512].rearrange("(k p) t -> p k t", p=128)

    def phase_merge(l):
        with ExitStack() as ph:
            walloc(ph)
            at = sb("mg_act", [128, 18, 1024], BF16, ph)
            ares = Res()
            glf = sb("mg_glf", [128, 2, 1024], F32, ph)
            glr = Res()
            sig = Ring(ph, "mg_sig", 3, [128, 512], F32)
            prod = Ring(ph, "mg_prod", 6, [128, 512], F32)
            acc = Ring(ph, "mg_acc", 2, [128, 512], F32)
            mo = Ring(ph, "mg_out", 3, [128, 512], BF16)
            srcs = ((yaT, 4, 2), (ybT, 4, 6), (ycT, 8, 10))
            wsl = ((w_br_a, 4, 6), (w_br_b, 4, 10), (w_br_c, 8, 14))
            for (c0, ncols, is_s) in cfg.tiles:
                NT = ncols // 512
                K.dma("sp", glf[:, :, 0:ncols], projT[4624:4880, c0:c0 + ncols].rearrange("(k p) t -> p k t", p=128),
                      writes=[glr])
                K.op("dve", lambda e, ncols=ncols: e.tensor_copy(out=at[:, 0:2, 0:ncols], in_=glf[:, :, 0:ncols]),
                     reads=[glr], writes=[ares])
                for (src, nch, a0) in srcs:
                    K.dma("sp", at[:, a0:a0 + nch, 0:ncols], src.rearrange("(k p) t -> p k t", p=128)[:, :, c0:c0 + ncols],
                          writes=[ares], disjoint=True)
                for bc0 in range(0, D, 256):
                    wi = wnext()
                    for b in range(3):
                        wload(wi, w_mg[l], 0, 2, b * D + bc0, 256, slot0=2 * b, view_cols=256)
                    for (wm, nch, s0) in wsl:
                        wload(wi, wm[l], 0, nch, bc0, 256, slot0=s0, view_cols=256)
                    wv = wview(wi, 32, 256)
                    for m in range(2):
                        for nt in range(NT):
                            prods = []
                            for b in range(3):
                                pg = bank()
                                pb = bank()
                                for k in range(2):
                                    K.op("pe", lambda e, pg=pg, b=b, k=k, m=m, nt=nt: e.matmul(
                                        psum[pg][:, :], wv[:, 2 * b + k, m * 128:(m + 1) * 128],
                                        at[:, k, nt * 512:(nt + 1) * 512], start=(k == 0), stop=(k == 1)),
                                        reads=[wres[wi], ares], writes=[psr[pg]])
                                nch, s0, a0 = wsl[b][1], wsl[b][2], srcs[b][2]
                                for k in range(nch):
                                    K.op("pe", lambda e, pb=pb, k=k, m=m, nt=nt, s0=s0, a0=a0, nch=nch: e.matmul(
                                        psum[pb][:, :], wv[:, s0 + k, m * 128:(m + 1) * 128],
                                        at[:, a0 + k, nt * 512:(nt + 1) * 512], start=(k == 0), stop=(k == nch - 1)),
                                        reads=[wres[wi], ares], writes=[psr[pb]])
                                sg_, sgr = sig.next()
                                pd_, pdr = prod.next()
                                K.op("act", lambda e, pg=pg, sg_=sg_: e.activation(out=sg_[:], in_=psum[pg][:, :], func=AF.Sigmoid),
                                     reads=[psr[pg]], writes=[sgr])
                                K.op("dve", lambda e, pb=pb, sg_=sg_, pd_=pd_: e.tensor_tensor(
                                    out=pd_[:], in0=sg_[:], in1=psum[pb][:, :], op=ALU.mult),
                                    reads=[sgr, psr[pb]], writes=[pdr])
                                prods.append((pd_, pdr))
                            ac_, acr = acc.next()
                            mo_, mor = mo.next()
                            K.op("pool", lambda e, ac_=ac_, prods=prods: e.tensor_tensor(
                                out=ac_[:], in0=prods[0][0][:], in1=prods[1][0][:], op=ALU.add),
                                reads=[prods[0][1], prods[1][1]], writes=[acr])
                            K.op("pool", lambda e, ac_=ac_, mo_=mo_, prods=prods: e.tensor_tensor(
                                out=mo_[:], in0=ac_[:], in1=prods[2][0][:], op=ALU.add),
                                reads=[acr, prods[2][1]], writes=[mor])
                            r0 = bc0 + m * 128
                            K.dma("sp", mrgT[r0:r0 + 128, c0 + nt * 512:c0 + (nt + 1) * 512], mo_[:], reads=[mor])
            K.barrier()

    def phase_dnet(l):
        seqs = []
        if LS:
            seqs.append((0, LS, True, 0))
        for pi in range(cfg.NP):
            seqs.append((LS + pi * LP, LP, False, pi))
        import os as _osc
        CUT = int(_osc.environ.get("D1_CUT", "0"))

        class StopEmit(Exception):
            pass

        def cut(n):
            if CUT == n:
                raise StopEmit()
        cutflag = [False]
        with ExitStack() as ph:
            try:
                mk = sb("d_mask", [128, 7, 512], F32, ph)
                cw = sb("d_cw", [128, 12, 5], F32, ph)
                abp = sb("d_abp", [16, 2], F32, ph)
                nexpA = sb("d_nexpA", [16, 1], F32, ph)
                one16 = sb("d_one16", [128, 1], F32, ph)
                epsc = sb("d_eps", [128, 1], F32, ph)
                cr = Res()
                K.dma("sp", mk[:], dmask.rearrange("s p f -> p s f"), writes=[cr])
                K.dma("sp", cw[:], b_convT[l], writes=[cr], disjoint=True)
                K.dma("sp", abp[:], ab_par[l], writes=[cr], disjoint=True)
                K.op("pool", lambda e: e.memset(one16[:], 1.0), writes=[cr])
                K.op("pool", lambda e: e.memset(epsc[:], EPS), writes=[cr])
                K.op("act", lambda e: e.activation(out=nexpA[:], in_=abp[:, 0:1], func=AF.Exp), reads=[cr], writes=[cr])
                K.op("dve", lambda e: e.tensor_scalar(out=nexpA[:], in0=nexpA[:], scalar1=-1.0, scalar2=None, op0=ALU.mult),
                     reads=[cr], writes=[cr])
                Mdir = (mk[:, 0, 0:128], mk[:, 0, 128:256])
                maskA = (mk[:, 1, :], mk[:, 2, :])
                maskB = (mk[:, 3, :], mk[:, 4, :])
                ident4 = mk[:, 5, :]
                onesF = mk[:, 6, 0:128]
                cvt = sb("d_cv", [128, 12, 512], F32, ph)
                cvrs = [Res() for _ in range(12)]
                xin = Ring(ph, "d_xin", 2, [128, 4, 516], F32)
                abt = sb("d_abt", [16, 512], F32, ph)
                e1 = sb("d_e1", [16, 512], F32, ph)
                gT = sb("d_gT", [16, 512], F32, ph)
                bT = sb("d_bT", [16, 512], F32, ph)
                gres = Res()
                lnt = sb("d_ln", [128, 512], F32, ph)
                T = Ring(ph, "d_T", 8, [128, 512], F32)
                DED = [{n: (sb("d_%s%d" % (n, dd), [128, 512], F32, ph), Res()) for n in ("x", "bv", "bgk", "p0", "p1", "pt0", "pt1")}
                       for dd in range(2)]
                OUT = Ring(ph, "d_OUT", 2, [128, 5, 512], F32)
                sm = Ring(ph, "d_sm", 12, [128, 32], F32)
                ktok = Ring(ph, "d_ktok", 2, [128, 512], F32)
                vtok = Ring(ph, "d_vtok", 2, [128, 512], F32)
                kks = Ring(ph, "d_kk", 2, [128, 512], F32)
                qks = Ring(ph, "d_qk", 2, [128, 512], F32)
                gtk = Ring(ph, "d_gtk", 2, [128, 32], F32)
                ncv = [0]

                def copy_ps(dst, b, wres_, rd=()):
                    ncv[0] += 1
                    if ncv[0] % 2:
                        K.op("act", lambda e: e.activation(out=dst, in_=psum[b][:, :], func=AF.Copy), reads=[psr[b]] + list(rd), writes=[wres_])
                    else:
                        K.op("dve", lambda e: e.tensor_copy(out=dst, in_=psum[b][:, :]), reads=[psr[b]] + list(rd), writes=[wres_])

                for si, (c0, ncols, is_s) in enumerate(cfg.sub):
                    if is_s:
                        segs = [(0, 512, 0, LS)]
                    else:
                        segs = [(j * LP, LP, c0 + j * LP, c0 + (j + 1) * LP) for j in range(512 // LP)]
                    nop = 0
                    for (s0, slen, lo, hi) in segs:
                        a = c0 + s0
                        va, vb = max(lo, a - 2), min(hi, a + slen + 2)
                        for grp in range(3):
                            xt_, xr_ = xin.next()
                            if va > a - 2 or vb < a + slen + 2:
                                K.op("pool", lambda e, xt_=xt_: e.memset(xt_[:], 0.0), writes=[xr_])
                            K.dma("sp", xt_[:, :, va - (a - 2):vb - (a - 2)], pTc(8 + grp * 4, 12 + grp * 4, va, vb), writes=[xr_])
                            for cc in range(4):
                                c = grp * 4 + cc
                                eng = "dve"
                                nop += 1
                                dst = cvt[:, c, s0:s0 + slen]
                                K.op(eng, lambda e, dst=dst, xt_=xt_, cc=cc, c=c, slen=slen: e.tensor_scalar(
                                    out=dst, in0=xt_[:, cc, 0:slen], scalar1=cw[:, c, 0:1], scalar2=None, op0=ALU.mult),
                                    reads=[xr_, cr], writes=[cvrs[c]])
                                for i in range(1, 5):
                                    K.op(eng, lambda e, dst=dst, xt_=xt_, cc=cc, c=c, i=i, slen=slen: e.scalar_tensor_tensor(
                                        out=dst, in0=xt_[:, cc, i:i + slen], scalar=cw[:, c, i:i + 1], in1=dst,
                                        op0=ALU.mult, op1=ALU.add), reads=[xr_, cr, cvrs[c]], writes=[cvrs[c]])
                    K.op("act", lambda e: e.activation(out=cvt[:], in_=cvt[:], func=AF.Silu), reads=cvrs, writes=cvrs)
                    cut(1)
                    for c in range(8):
                        sq_, sqr_ = T.next()
                        K.op("pool", lambda e, c=c, sq_=sq_: e.tensor_tensor(out=sq_[:], in0=cvt[:, c, :], in1=cvt[:, c, :], op=ALU.mult),
                             reads=[cvrs[c]], writes=[sqr_])
                        b = bank()
                        K.op("pe", lambda e, b=b, sq_=sq_: e.matmul(psum[b][:, :], onesF, sq_[:], start=True, stop=True),
                             reads=[sqr_, cr], writes=[psr[b]])
                        rs_, rsr_ = T.next()
                        rstd_from_ps(b, lnt, rs_[:], rsr_, epsc)
                        sc = (128.0 ** -0.5) if c < 4 else 1.0
                        K.op("dve", lambda e, c=c, rs_=rs_, sc=sc: e.scalar_tensor_tensor(
                            out=cvt[:, c, :], in0=cvt[:, c, :], scalar=sc, in1=rs_[:], op0=ALU.mult, op1=ALU.mult),
                            reads=[cvrs[c], rsr_], writes=[cvrs[c]])
                    cut(2)
                    K.dma("sp", abt[:], projT[3072:3088, c0:c0 + 512], writes=[gres])
                    K.op("act", lambda e: e.activation(out=e1[:], in_=abt[:], func=AF.Exp, bias=abp[:, 1:2]), reads=[gres, cr], writes=[gres])
                    K.op("act", lambda e: e.activation(out=e1[:], in_=e1[:], func=AF.Ln, bias=one16[0:16, :]), reads=[gres, cr], writes=[gres])
                    K.op("dve", lambda e: e.tensor_scalar(out=gT[:], in0=e1[:], scalar1=nexpA[:, 0:1], scalar2=None, op0=ALU.mult),
                         reads=[gres, cr], writes=[gres])
                    K.op("act", lambda e: e.activation(out=bT[:], in_=abt[:], func=AF.Exp, scale=-1.0), reads=[gres], writes=[gres])
                    K.op("dve", lambda e: e.tensor_scalar(out=bT[:], in0=bT[:], scalar1=1.0, scalar2=None, op0=ALU.add), reads=[gres], writes=[gres])
                    K.op("dve", lambda e: e.reciprocal(out=bT[:], in_=bT[:]), reads=[gres], writes=[gres])
                    cut(3)
                    for j in range(4):
                        cg = (c0 + j * 128) // 128
                        cs = slice(j * 128, (j + 1) * 128)
                        kt_, ktr = ktok.next()
                        vt_, vtr = vtok.next()
                        for (dst, dres, ch0) in ((kt_, ktr, 4), (vt_, vtr, 8)):
                            b = bank()
                            for h in range(4):
                                K.op("pe", lambda e, b=b, h=h, ch0=ch0, cs=cs: e.transpose(
                                    psum[b][:, h * 128:(h + 1) * 128], cvt[:, ch0 + h, cs], ident[:]),
                                    reads=[cvrs[ch0 + h], cres], writes=[psr[b]])
                            copy_ps(dst[:], b, dres)
                        gk_, gkr = gtk.next()
                        b = bank()
                        K.op("pe", lambda e, b=b, cs=cs: e.transpose(psum[b][:, 0:16], gT[0:16, cs], ident[0:16, 0:16]),
                             reads=[gres, cres], writes=[psr[b]])
                        K.op("pe", lambda e, b=b, cs=cs: e.transpose(psum[b][:, 16:32], bT[0:16, cs], ident[0:16, 0:16]),
                             reads=[gres, cres], writes=[psr[b]])
                        K.op("dve", lambda e, b=b, gk_=gk_: e.tensor_copy(out=gk_[:], in_=psum[b][:, 0:32]), reads=[psr[b]], writes=[gkr])
                        kk_, kkr = kks.next()
                        qk_, qkr = qks.next()
                        b = bank()
                        for h in range(4):
                            K.op("pe", lambda e, b=b, h=h, cs=cs: e.matmul(psum[b][:, h * 128:(h + 1) * 128], cvt[:, 4 + h, cs], cvt[:, 4 + h, cs],
                                                                         start=True, stop=True), reads=[cvrs[4 + h]], writes=[psr[b]])
                        copy_ps(kk_[:], b, kkr)
                        b = bank()
                        for h in range(4):
                            K.op("pe", lambda e, b=b, h=h, cs=cs: e.matmul(psum[b][:, h * 128:(h + 1) * 128], cvt[:, 4 + h, cs], cvt[:, h, cs],
                                                                         start=True, stop=True), reads=[cvrs[4 + h], cvrs[h]], writes=[psr[b]])
                        copy_ps(qk_[:], b, qkr)
                        cut(4)
                        units = []
                        for dd in range(2):
                            last = 127 if dd == 0 else 0
                            g4 = gk_[:, dd * 4:(dd + 1) * 4]
                            beta4 = gk_[:, 24 + dd * 4:24 + (dd + 1) * 4]
                            s1, s1r = sm.next()
                            b = bank()
                            K.op("pe", lambda e, b=b, dd=dd, g4=g4: e.matmul(psum[b][:, 0:4], Mdir[dd], g4, start=True, stop=True),
                                 reads=[gkr, cr], writes=[psr[b]])
                            K.op("dve", lambda e, b=b, s1=s1: e.tensor_copy(out=s1[:, 0:4], in_=psum[b][:, 0:4]), reads=[psr[b]], writes=[s1r])
                            K.op("dve", lambda e, s1=s1: e.tensor_scalar(out=s1[:, 4:8], in0=s1[:, 0:4], scalar1=-1.0, scalar2=None, op0=ALU.mult),
                                 reads=[s1r], writes=[s1r])
                            K.op("act", lambda e, s1=s1: e.activation(out=s1[:, 8:12], in_=s1[:, 0:4], func=AF.Exp), reads=[s1r], writes=[s1r])
                            K.op("dve", lambda e, s1=s1, beta4=beta4: e.tensor_tensor(out=s1[:, 8:12], in0=s1[:, 8:12], in1=beta4, op=ALU.mult),
                                 reads=[s1r, gkr], writes=[s1r])
                            K.op("dve", lambda e, s1=s1, beta4=beta4: e.tensor_scalar(out=s1[:, 12:16], in0=beta4, scalar1=-1.0, scalar2=None, op0=ALU.mult),
                                 reads=[gkr], writes=[s1r])
                            cut(41)
                            mg_, mgr = T.next()
                            for h in range(4):
                                K.op("pool", lambda e, h=h, mg_=mg_, dd=dd, g4=g4: e.tensor_scalar(
                                    out=mg_[:, h * 128:(h + 1) * 128], in0=Mdir[dd], scalar1=g4[:, h:h + 1], scalar2=None, op0=ALU.mult),
                                    reads=[gkr, cr], writes=[mgr])
                            bU, bA, bB = bank(), bank(), bank()
                            K.op("pe", lambda e, bU=bU, mg_=mg_: e.matmul(psum[bU][:, :], onesF, mg_[:], start=True, stop=True),
                                 reads=[mgr, cr], writes=[psr[bU]])
                            K.op("pe", lambda e, bA=bA, mg_=mg_: e.matmul(psum[bA][:, :], onesF, mg_[:], start=True, stop=False),
                                 reads=[mgr, cr], writes=[psr[bA]])
                            K.op("pe", lambda e, bA=bA, dd=dd: e.matmul(psum[bA][:, :], ident[:], maskA[dd], start=False, stop=True),
                                 reads=[cr, cres], writes=[psr[bA]])
                            K.op("pe", lambda e, bB=bB, mg_=mg_: e.matmul(psum[bB][:, :], onesF, mg_[:], start=True, stop=False),
                                 reads=[mgr, cr], writes=[psr[bB]])
                            K.op("pe", lambda e, bB=bB, dd=dd: e.matmul(psum[bB][:, :], ident[:], maskB[dd], start=False, stop=True),
                                 reads=[cr, cres], writes=[psr[bB]])
                            cut(42)
                            ot_, otr = OUT.next()
                            gam_, gamr = T.next()
                            K.op("act", lambda e, bU=bU, gam_=gam_: e.activation(out=gam_[:], in_=psum[bU][:, :], func=AF.Exp),
                                 reads=[psr[bU]], writes=[gamr])
                            K.op("act", lambda e, bU=bU, s1=s1, last=last: e.activation(
                                out=s1[:, 16:20], in_=psum[bU][:, :].rearrange("p (h t) -> p h t", t=128)[:, :, last], func=AF.Exp),
                                reads=[psr[bU]], writes=[s1r])
                            cut(43)
                            for h in range(4):
                                K.op("act", lambda e, bU=bU, s1=s1, h=h, last=last: e.activation(
                                    out=s1[:, 20 + h:21 + h], in_=psum[bU][:, h * 128 + last:h * 128 + last + 1], func=AF.Exp,
                                    bias=s1[:, 4 + h:5 + h]), reads=[psr[bU], s1r], writes=[s1r])
                            cut(44)
                            dl_, dlr = T.next()
                            dt_, dtr = T.next()
                            for h in range(4):
                                hs = slice(h * 128, (h + 1) * 128)
                                K.op("act", lambda e, bA=bA, dl_=dl_, s1=s1, h=h, hs=hs: e.activation(
                                    out=dl_[:, hs], in_=psum[bA][:, hs], func=AF.Exp, bias=s1[:, h:h + 1], scale=-1.0),
                                    reads=[psr[bA], s1r], writes=[dlr])
                                K.op("act", lambda e, bB=bB, dt_=dt_, s1=s1, h=h, hs=hs: e.activation(
                                    out=dt_[:, hs], in_=psum[bB][:, hs], func=AF.Exp, bias=s1[:, 4 + h:5 + h], scale=1.0),
                                    reads=[psr[bB], s1r], writes=[dtr])
                            cut(45)
                            pt_, ptr_ = DED[dd]["pt0"]
                            for h in range(4):
                                hs = slice(h * 128, (h + 1) * 128)
                                K.op("dve", lambda e, pt_=pt_, kk_=kk_, s1=s1, dl_=dl_, h=h, hs=hs: e.scalar_tensor_tensor(
                                    out=pt_[:, hs], in0=kk_[:, hs], scalar=s1[:, 12 + h:13 + h], in1=dl_[:, hs], op0=ALU.mult, op1=ALU.mult),
                                    reads=[kkr, s1r, dlr], writes=[ptr_])
                            cut(46)
                            K.op("dve", lambda e, ot_=ot_, gam_=gam_, cs=cs: e.tensor_tensor(
                                out=ot_[:, 2, :].rearrange("p (h t) -> p h t", t=128), in0=cvt[:, 0:4, cs],
                                in1=gam_[:].rearrange("p (h t) -> p h t", t=128), op=ALU.mult), reads=cvrs[0:4] + [gamr], writes=[otr])
                            K.op("pool", lambda e, ot_=ot_, qk_=qk_, dt_=dt_: e.tensor_tensor(out=ot_[:, 3, :], in0=qk_[:], in1=dt_[:], op=ALU.mult),
                                 reads=[qkr, dtr], writes=[otr])
                            bv_, bvr = DED[dd]["bv"]
                            bgk_, bgkr = DED[dd]["bgk"]
                            for h in range(4):
                                hs = slice(h * 128, (h + 1) * 128)
                                K.op("pool", lambda e, ot_=ot_, kt_=kt_, s1=s1, h=h, hs=hs: e.tensor_scalar(
                                    out=ot_[:, 4, hs], in0=kt_[:, hs], scalar1=s1[:, 20 + h:21 + h], scalar2=None, op0=ALU.mult),
                                    reads=[ktr, s1r], writes=[otr])
                                K.op("pool", lambda e, bv_=bv_, vt_=vt_, beta4=beta4, h=h, hs=hs: e.tensor_scalar(
                                    out=bv_[:, hs], in0=vt_[:, hs], scalar1=beta4[:, h:h + 1], scalar2=None, op0=ALU.mult),
                                    reads=[vtr, gkr], writes=[bvr])
                                K.op("pool", lambda e, bgk_=bgk_, kt_=kt_, s1=s1, h=h, hs=hs: e.tensor_scalar(
                                    out=bgk_[:, hs], in0=kt_[:, hs], scalar1=s1[:, 8 + h:9 + h], scalar2=None, op0=ALU.mult),
                                    reads=[ktr, s1r], writes=[bgkr])
                            cut(47)
                            units.append(dict(dd=dd, pt=(pt_, ptr_), ot=(ot_, otr), bv=(bv_, bvr), bgk=(bgk_, bgkr), s1=(s1, s1r), cg=cg))
                        cut(5)
                        for u in units:
                            pt_, ptr_ = u["pt"]
                            b = bank()
                            for h in range(4):
                                hs = slice(h * 128, (h + 1) * 128)
                                K.op("pe", lambda e, b=b, pt_=pt_, hs=hs: e.transpose(psum[b][:, hs], pt_[:, hs], ident[:]),
                                     reads=[ptr_, cres], writes=[psr[b]])
                            p_, pr_ = DED[u["dd"]]["p0"]
                            x_, xr_ = DED[u["dd"]]["x"]
                            cut(51)
                            K.op("act", lambda e, b=b, p_=p_: e.activation(out=p_[:], in_=psum[b][:, :], func=AF.Copy), reads=[psr[b]], writes=[pr_])
                            cut(52)
                            K.op("dve", lambda e, p_=p_, x_=x_: e.tensor_tensor(out=x_[:], in0=p_[:], in1=ident4, op=ALU.add),
                                 reads=[pr_, cr], writes=[xr_])
                            u["p"] = (p_, pr_)
                            u["x"] = (x_, xr_)
                        cut(6)
                        for k in range(1, 7):
                            for u in units:
                                p_, pr_ = u["p"]
                                pt_, ptr_ = u["pt"]
                                x_, xr_ = u["x"]
                                if k < 6:
                                    b = bank()
                                    for h in range(4):
                                        hs = slice(h * 128, (h + 1) * 128)
                                        K.op("pe", lambda e, b=b, pt_=pt_, p_=p_, hs=hs: e.matmul(psum[b][:, hs], pt_[:, hs], p_[:, hs], start=True, stop=True),
                                             reads=[ptr_, pr_], writes=[psr[b]])
                                    pn_, pnr_ = DED[u["dd"]]["p%d" % (k % 2)]
                                    K.op("act", lambda e, b=b, pn_=pn_: e.activation(out=pn_[:], in_=psum[b][:, :], func=AF.Copy), reads=[psr[b]], writes=[pnr_])
                                b2 = bank()
                                for h in range(4):
                                    hs = slice(h * 128, (h + 1) * 128)
                                    K.op("pe", lambda e, b2=b2, pt_=pt_, p_=p_, hs=hs: e.matmul(psum[b2][:, hs], p_[:, hs], pt_[:, hs], start=True, stop=True),
                                         reads=[ptr_, pr_], writes=[psr[b2]])
                                ptn_, ptnr_ = DED[u["dd"]]["pt%d" % (k % 2)]
                                K.op("dve", lambda e, b2=b2, ptn_=ptn_: e.tensor_copy(out=ptn_[:], in_=psum[b2][:, :]), reads=[psr[b2]], writes=[ptnr_])
                                b3 = bank()
                                for h in range(4):
                                    hs = slice(h * 128, (h + 1) * 128)
                                    K.op("pe", lambda e, b3=b3, ptn_=ptn_, x_=x_, hs=hs: e.matmul(psum[b3][:, hs], ptn_[:, hs], x_[:, hs], start=True, stop=True),
                                         reads=[ptnr_, xr_], writes=[psr[b3]])
                                K.op("dve", lambda e, b3=b3, x_=x_: e.tensor_tensor(out=x_[:], in0=x_[:], in1=psum[b3][:, :], op=ALU.add),
                                     reads=[psr[b3], xr_], writes=[xr_])
                                if k < 6:
                                    u["p"] = (pn_, pnr_)
                                u["pt"] = (ptn_, ptnr_)
                        cut(7)
                        for u in units:
                            x_, xr_ = u["x"]
                            ot_, otr = u["ot"]
                            bv_, bvr = u["bv"]
                            bgk_, bgkr = u["bgk"]
                            s1, s1r = u["s1"]
                            b = bank()
                            for h in range(4):
                                hs = slice(h * 128, (h + 1) * 128)
                                K.op("pe", lambda e, b=b, bgk_=bgk_, x_=x_, hs=hs: e.matmul(psum[b][:, hs], bgk_[:, hs], x_[:, hs], start=True, stop=True),
                                     reads=[bgkr, xr_], writes=[psr[b]])
                            K.op("act", lambda e, b=b, ot_=ot_: e.activation(out=ot_[:, 0, :], in_=psum[b][:, :], func=AF.Copy), reads=[psr[b]], writes=[otr])
                            b = bank()
                            for h in range(4):
                                hs = slice(h * 128, (h + 1) * 128)
                                K.op("pe", lambda e, b=b, bv_=bv_, x_=x_, hs=hs: e.matmul(psum[b][:, hs], x_[:, hs], bv_[:, hs], start=True, stop=True),
                                     reads=[bvr, xr_], writes=[psr[b]])
                            K.op("dve", lambda e, b=b, ot_=ot_: e.tensor_copy(out=ot_[:, 1, :], in_=psum[b][:, :]), reads=[psr[b]], writes=[otr])
                            K.dma("sp", dn_scr[u["cg"], u["dd"]].rearrange("s p f -> p s f"), ot_[:], reads=[otr])
                            K.dma("sp", dn_ge[u["cg"], u["dd"]], s1[:, 16:20], reads=[s1r])
                        cut(8)
                K.barrier()
            except StopEmit:
                K.barrier()
                cutflag[0] = True
        if cutflag[0]:
            return "cut"
        import os as _os
        if int(_os.environ.get("DN_STOP", "9")) < 2:
            return
        with ExitStack() as ph:
            S = [sb("d_S%d" % dd, [128, 4, 128], F32, ph) for dd in range(2)]
            Sr = [Res() for _ in range(2)]
            IN = [Ring(ph, "d_IN%d" % dd, 2, [128, 5, 512], F32) for dd in range(2)]
            GE = [Ring(ph, "d_GE%d" % dd, 2, [128, 4], F32) for dd in range(2)]
            U = Ring(ph, "d_U", 4, [128, 512], F32)
            OO = Ring(ph, "d_OO", 4, [128, 512], F32)
            for (t0, slen, is_s, pi) in seqs:
                nch = slen // 128
                for dd in range(2):
                    if is_s:
                        src = (sf_in if dd == 0 else sb_in)[l].rearrange("h k v -> k h v")
                        K.dma("sp", S[dd][:], src, writes=[Sr[dd]])
                    else:
                        K.op("pool", lambda e, dd=dd: e.memset(S[dd][:], 0.0), writes=[Sr[dd]])
                for step in range(nch):
                    for dd in range(2):
                        ch = step if dd == 0 else nch - 1 - step
                        cg = t0 // 128 + ch
                        in_, inr = IN[dd].next()
                        ge_, ger = GE[dd].next()
                        K.dma("sp", in_[:], dn_scr[cg, dd].rearrange("s p f -> p s f"), writes=[inr])
                        K.dma("sp", ge_[:], dn_ge[cg, dd], writes=[ger])
                        bA = bank()
                        for h in range(4):
                            hs = slice(h * 128, (h + 1) * 128)
                            K.op("pe", lambda e, bA=bA, in_=in_, dd=dd, h=h, hs=hs: e.matmul(psum[bA][:, hs], in_[:, 0, hs], S[dd][:, h, :], start=True, stop=True),
                                 reads=[inr, Sr[dd]], writes=[psr[bA]])
                        u_, ur_ = U.next()
                        K.op("dve", lambda e, bA=bA, in_=in_, u_=u_: e.tensor_tensor(out=u_[:], in0=in_[:, 1, :], in1=psum[bA][:, :], op=ALU.subtract),
                             reads=[inr, psr[bA]], writes=[ur_])
                        bB = bank()
                        for h in range(4):
                            hs = slice(h * 128, (h + 1) * 128)
                            K.op("pe", lambda e, bB=bB, in_=in_, dd=dd, h=h, hs=hs: e.matmul(psum[bB][:, hs], in_[:, 2, hs], S[dd][:, h, :], start=True, stop=False),
                                 reads=[inr, Sr[dd]], writes=[psr[bB]])
                            K.op("pe", lambda e, bB=bB, in_=in_, u_=u_, hs=hs: e.matmul(psum[bB][:, hs], in_[:, 3, hs], u_[:, hs], start=False, stop=True),
                                 reads=[inr, ur_], writes=[psr[bB]])
                        bC = bank()
                        for h in range(4):
                            hs = slice(h * 128, (h + 1) * 128)
                            K.op("pe", lambda e, bC=bC, in_=in_, u_=u_, hs=hs: e.matmul(psum[bC][:, hs], in_[:, 4, hs], u_[:, hs], start=True, stop=True),
                                 reads=[inr, ur_], writes=[psr[bC]])
                        o_, or_ = OO.next()
                        K.op("act", lambda e, bB=bB, o_=o_: e.activation(out=o_[:], in_=psum[bB][:, :], func=AF.Copy), reads=[psr[bB]], writes=[or_])
                        K.dma("sp", Osc[dd, cg * 128:(cg + 1) * 128, :], o_[:], reads=[or_])
                        for h in range(4):
                            hs = slice(h * 128, (h + 1) * 128)
                            K.op("dve", lambda e, bC=bC, dd=dd, h=h, hs=hs, ge_=ge_: e.scalar_tensor_tensor(
                                out=S[dd][:, h, :], in0=S[dd][:, h, :], scalar=ge_[:, h:h + 1], in1=psum[bC][:, hs], op0=ALU.mult, op1=ALU.add),
                                reads=[Sr[dd], ger, psr[bC]], writes=[Sr[dd]])
                if not is_s:
                    for dd in range(2):
                        dst = (new_sf if dd == 0 else new_sb)[pi, l].rearrange("h k v -> k h v")
                        K.dma("sp", dst, S[dd][:], reads=[Sr[dd]])
            K.barrier()
        if int(_os.environ.get("DN_STOP", "9")) < 3:
            return
        with ExitStack() as ph:
            pvt = sb("f_pv", [128, 8], F32, ph)
            epsc = sb("f_eps", [128, 1], F32, ph)
            cr = Res()
            K.dma("sp", pvt[:], pv[l], writes=[cr])
            K.op("pool", lambda e: e.memset(epsc[:], EPS), writes=[cr])
            gt = Ring(ph, "f_gt", 2, [128, 4, 512], F32)
            of = Ring(ph, "f_of", 3, [128, 512], F32)
            ob = Ring(ph, "f_ob", 3, [128, 512], F32)
            sq = Ring(ph, "f_sq", 2, [128, 512], F32)
            sm = Ring(ph, "f_sm", 3, [128, 8], F32)
            yb = Ring(ph, "f_yb", 2, [128, 4, 512], BF16)
            ybTv = ybT.rearrange("(k p) t -> p k t", p=128)
            for (c0, ncols, is_s) in cfg.sub:
                g_, gr_ = gt.next()
                K.dma("sp", g_[:], pTc(20, 24, c0, c0 + 512), writes=[gr_])
                K.op("act", lambda e, g_=g_: e.activation(out=g_[:], in_=g_[:], func=AF.Silu), reads=[gr_], writes=[gr_])
                y_, yr_ = yb.next()
                for j in range(4):
                    tk = c0 + j * 128
                    f_, fr_ = of.next()
                    b_, br_ = ob.next()
                    K.dma("sp", f_[:], Osc[0, tk:tk + 128, :], writes=[fr_])
                    K.dma("sp", b_[:], Osc[1, tk:tk + 128, :], writes=[br_])
                    K.op("pool", lambda e, f_=f_, b_=b_: e.tensor_tensor(out=f_[:], in0=f_[:], in1=b_[:], op=ALU.add), reads=[fr_, br_], writes=[fr_])
                    q_, qr_ = sq.next()
                    K.op("pool", lambda e, f_=f_, q_=q_: e.tensor_tensor(out=q_[:], in0=f_[:], in1=f_[:], op=ALU.mult), reads=[fr_], writes=[qr_])
                    s_, sr_ = sm.next()
                    K.op("dve", lambda e, s_=s_, q_=q_: e.reduce_sum(out=s_[:, 0:4], in_=q_[:].rearrange("p (h e) -> p h e", e=128),
                                                                   axis=mybir.AxisListType.X), reads=[qr_], writes=[sr_])
                    K.op("act", lambda e, s_=s_: e.activation(out=s_[:, 4:8], in_=s_[:, 0:4], func=AF.Ln, bias=epsc[:], scale=1.0 / 128),
                         reads=[sr_, cr], writes=[sr_])
                    K.op("act", lambda e, s_=s_: e.activation(out=s_[:, 0:4], in_=s_[:, 4:8], func=AF.Exp, scale=-0.5), reads=[sr_], writes=[sr_])
                    for h in range(4):
                        hs = slice(h * 128, (h + 1) * 128)
                        K.op("dve", lambda e, f_=f_, s_=s_, h=h, hs=hs: e.tensor_scalar(out=f_[:, hs], in0=f_[:, hs], scalar1=s_[:, h:h + 1], scalar2=None,
                                                                                      op0=ALU.mult), reads=[fr_, sr_], writes=[fr_])
                    bt = bank()
                    for h in range(4):
                        hs = slice(h * 128, (h + 1) * 128)
                        K.op("pe", lambda e, bt=bt, f_=f_, hs=hs: e.transpose(psum[bt][:, hs], f_[:, hs], ident[:]), reads=[fr_, cres], writes=[psr[bt]])
                    K.op("dve", lambda e, bt=bt, y_=y_, g_=g_, j=j: e.scalar_tensor_tensor(
                        out=y_[:, :, j * 128:(j + 1) * 128], in0=psum[bt][:, :].rearrange("p (h t) -> p h t", t=128), scalar=pvt[:, 4:5],
                        in1=g_[:, :, j * 128:(j + 1) * 128], op0=ALU.mult, op1=ALU.mult), reads=[psr[bt], gr_, cr], writes=[yr_])
                K.dma("sp", ybTv[:, 0:4, c0:c0 + 512], y_[:], reads=[yr_])
            K.barrier()

    import os as _os2
    if _os2.environ.get("ONLY_DNET"):
        if phase_dnet(0) != "cut":
            phase_transpose_out()
        es.close()
        return nc
    phase_transpose_in()
    for l in range(L):
        phase_mod(l)
        phase_norm(0, 1)
        if cfg.nphase >= 2:
            phase_proj(l)
        if cfg.nphase >= 3:
            phase_gmlp(l)
            phase_attn(l)
        if cfg.nphase >= 4:
            phase_dnet(l)
        if cfg.nphase >= 5:
            phase_merge(l)
            phase_resid(l, mrgT, KC, w_o[l], 2, "wo")
        if cfg.nphase >= 6:
            phase_norm(3, 4)
            phase_gateup(l)
            phase_resid(l, hidT, FC, w_down[l], 5, "dn")
    phase_transpose_out()
    es.close()
    return nc


BIGM = 30000.0


def host_consts(LS):
    c = {}
    c["ident"] = np.eye(128, dtype=np.float32)
    n = max(LS, 64)
    rows = n // 64
    row = np.repeat(np.arange(rows, dtype=np.float32), 64)
    col = np.tile(np.arange(64, dtype=np.float32), rows)
    inv = (1.0 / (np.float32(10000.0) ** (np.arange(0, 64, 2, dtype=np.float32) / np.float32(64)))).astype(np.float32)
    ang = np.stack([row[:, None] * inv, col[:, None] * inv], axis=1).astype(np.float32)
    cos, sin = np.cos(ang), np.sin(ang)
    cT = np.zeros((128, n), np.float32)
    sT = np.zeros((128, n), np.float32)
    perm = np.zeros((128, 128), np.float32)
    for a in range(2):
        for b in range(2):
            for i in range(32):
                p = a * 64 + b * 32 + i
                cT[p] = cos[:, a, i]
                sT[p] = sin[:, a, i] * (-1.0 if b == 0 else 1.0)
                perm[a * 64 + (1 - b) * 32 + i, p] = 1.0
    c["ropeC"] = np.ascontiguousarray(cT[:, :LS])
    c["ropeS"] = np.ascontiguousarray(sT[:, :LS])
    c["perm"] = perm
    j = np.arange(128)[:, None]
    t = np.arange(128)[None, :]
    dm = np.zeros((7, 128, 512), np.float32)
    dm[0, :, 0:128] = (j <= t)
    dm[0, :, 128:256] = (j >= t)
    dm[1] = np.tile(BIGM * (t >= j), (1, 4))
    dm[2] = np.tile(BIGM * (t <= j), (1, 4))
    dm[3] = np.tile(-BIGM * (j > t), (1, 4))
    dm[4] = np.tile(-BIGM * (j < t), (1, 4))
    dm[5] = np.tile(np.eye(128, dtype=np.float32), (1, 4))
    dm[6] = 1.0
    c["dmask"] = dm.astype(np.float32)
    return c


def host_params(inp):
    L = inp["w_in"].shape[0]
    f = lambda a: np.ascontiguousarray(np.asarray(a, dtype=np.float32))
    o = {}
    o["b_adaT"] = f(np.asarray(inp["b_ada"]).reshape(L, 192, 128).transpose(0, 2, 1))
    o["a_wsT"] = f(np.asarray(inp["a_w_s"]).transpose(0, 3, 1, 2))
    o["bs_row"] = f(np.asarray(inp["a_b_s"]).reshape(L, 1, 512))
    pvv = np.zeros((L, 128, 8), np.float32)
    pvv[:, :, 0:4] = np.asarray(inp["a_v_gain"]).reshape(L, 4, 128).transpose(0, 2, 1)
    pvv[:, :, 4] = np.asarray(inp["b_o_gain"])
    pvv[:, :, 5] = np.asarray(inp["c_q_gain"])
    pvv[:, :, 6] = np.asarray(inp["c_k_gain"])
    o["pv"] = pvv
    o["b_convT"] = f(np.asarray(inp["b_conv"]).reshape(L, 5, 12, 128).transpose(0, 3, 2, 1))
    abp = np.zeros((L, 16, 2), np.float32)
    abp[:, 0:8, 0] = np.asarray(inp["b_a_log"]).reshape(L, 8)
    abp[:, 0:8, 1] = np.asarray(inp["b_dt_bias"]).reshape(L, 8)
    o["ab_par"] = abp
    for k in ("w_ada", "w_in", "w_mg", "w_br_a", "w_br_b", "w_br_c", "w_o", "w_gate", "w_up", "w_down"):
        o[k] = f(inp[k])
    return o


def core_inputs(inp, shared, consts, core, n_prompt_per_core):
    f = lambda a: np.ascontiguousarray(np.asarray(a, dtype=np.float32))
    L = inp["w_in"].shape[0]
    m = dict(shared)
    m.update(consts)
    m["xs"] = f(inp["x_sample"][core])
    p0 = core * n_prompt_per_core
    m["xp"] = f(np.asarray(inp["x_prompt"][p0:p0 + n_prompt_per_core]).reshape(-1, D))
    c_ctx = np.asarray(inp["c_ctx"]).reshape(32, 128).T
    cc = np.asarray(inp["c"][core]).reshape(32, 128).T
    m["condT"] = f(np.stack([c_ctx, cc], axis=-1))
    m["ck"] = f(np.asarray(inp["cache_k"][core]).reshape(L, 256, 256))
    m["cv"] = f(np.asarray(inp["cache_v"][core]).reshape(L, 256, 256))
    m["sf_in"] = f(inp["state_fwd"][core])
    m["sb_in"] = f(inp["state_bwd"][core])
    return m


_NC_CACHE = {}


def kernel(**inputs):
    n = 8
    LS = inputs["x_sample"].shape[1]
    depth = inputs["w_in"].shape[0]
    npc = inputs["x_prompt"].shape[0] // n
    cfg = Cfg(depth=depth, LS=LS, NP=npc)
    key = (depth, LS, npc)
    if key not in _NC_CACHE:
        _NC_CACHE[key] = build(cfg)
    nc = _NC_CACHE[key]
    shared = host_params(inputs)
    consts = host_consts(LS)
    in_maps = [core_inputs(inputs, shared, consts, c, npc) for c in range(n)]
    res = run_bass_kernel_spmd(nc, in_maps, core_ids=list(range(n)))
    r = res.results
    B = inputs["x_prompt"].shape[0]
    y_sample = np.stack([r[c]["ys"] for c in range(n)], 0).astype(np.float32)
    y_prompt = np.concatenate([r[c]["yp"].reshape(npc, LP, D) for c in range(n)], 0).astype(np.float32)
    nk = np.concatenate([r[c]["new_k"].reshape(npc, depth, LP, 2, 128) for c in range(n)], 0).astype(np.float32)
    nv = np.concatenate([r[c]["new_v"].reshape(npc, depth, LP, 2, 128) for c in range(n)], 0).astype(np.float32)
    nsf = np.concatenate([r[c]["new_sf"] for c in range(n)], 0).astype(np.float32)
    nsb = np.concatenate([r[c]["new_sb"] for c in range(n)], 0).astype(np.float32)
    return (y_prompt, y_sample, nk, nv, nsf, nsb)
```
